# Optimizing a Trainium2 kernel written in Bass

```python
import jax, jax.numpy as jnp
from jax import lax
import numpy as np

D_MODEL = 1024
BATCH = 8
SEQ = 4096
DEPTH = 1

HEAD_DIM = 64
A_HEADS = D_MODEL // (2 * HEAD_DIM)
A_KV_HEADS = max(1, A_HEADS // 4)
A_WINDOW = 128
B_HEADS = D_MODEL // (2 * HEAD_DIM)
B_PATTERNS = ((128, 1), (512, 4), (2048, 16))
BLOCK = 128
ROPE_THETA = 10000.0
EPS = 1e-6
NEG = -1e30

A_WIDTH = A_HEADS * HEAD_DIM
A_KV_WIDTH = A_KV_HEADS * HEAD_DIM
B_WIDTH = B_HEADS * HEAD_DIM
MIX_WIDTH = A_WIDTH + B_WIDTH
IN_SPLITS = (A_WIDTH, A_KV_WIDTH, A_KV_WIDTH, A_WIDTH, B_WIDTH, B_WIDTH, B_WIDTH, B_WIDTH)
IN_WIDTH = sum(IN_SPLITS)

kernel_name = "hybrid_swa_sink_dilated_gated"


def rmsnorm(t, gain):
    tf = t.astype(jnp.float32)
    tf = tf * lax.rsqrt(jnp.mean(tf * tf, axis=-1, keepdims=True) + EPS)
    return (tf * gain.astype(jnp.float32)).astype(t.dtype)


def rope(t, pos):
    half = HEAD_DIM // 2
    inv = ROPE_THETA ** (-jnp.arange(half, dtype=jnp.float32) / half)
    ang = pos.astype(jnp.float32)[:, None] * inv[None, :]
    cos = jnp.cos(ang)[:, None, :]
    sin = jnp.sin(ang)[:, None, :]
    tf = t.astype(jnp.float32)
    t1, t2 = tf[..., :half], tf[..., half:]
    return jnp.concatenate([t1 * cos - t2 * sin, t2 * cos + t1 * sin], axis=-1).astype(t.dtype)


def banded_attention(q, k, v, max_dist, sinks=None):
    n, L, h, d = q.shape
    hkv = k.shape[2]
    g = h // hkv
    nb = -(-L // BLOCK)
    lp = nb * BLOCK
    pad = lp - L
    q = jnp.pad(q, ((0, 0), (0, pad), (0, 0), (0, 0)))
    k = jnp.pad(k, ((0, 0), (BLOCK, pad), (0, 0), (0, 0)))
    v = jnp.pad(v, ((0, 0), (BLOCK, pad), (0, 0), (0, 0)))
    qb = q.reshape(n, nb, BLOCK, hkv, g, d)
    kb = k.reshape(n, nb + 1, BLOCK, hkv, d)
    vb = v.reshape(n, nb + 1, BLOCK, hkv, d)
    kw = jnp.concatenate([kb[:, :-1], kb[:, 1:]], axis=2)
    vw = jnp.concatenate([vb[:, :-1], vb[:, 1:]], axis=2)
    s = jnp.einsum("nbqkgd,nbskd->nbkgqs", qb, kw,
                   preferred_element_type=jnp.float32) * (d ** -0.5)
    qi = jnp.arange(BLOCK)[:, None]
    sj = jnp.arange(2 * BLOCK)[None, :]
    dist = qi - sj + BLOCK
    key_pos = jnp.arange(nb)[:, None] * BLOCK - BLOCK + sj
    valid = ((dist >= 0) & (dist <= max_dist))[None] & (key_pos >= 0)[:, None, :]
    s = jnp.where(valid[None, :, None, None], s, NEG)
    m = s.max(axis=-1)
    if sinks is not None:
        sk = sinks.astype(jnp.float32).reshape(hkv, g)[None, None, :, :, None]
        m = jnp.maximum(m, sk)
    p = jnp.exp(s - m[..., None])
    l = p.sum(axis=-1)
    if sinks is not None:
        l = l + jnp.exp(sk - m)
    o = jnp.einsum("nbkgqs,nbskd->nbqkgd", p, vw.astype(jnp.float32))
    m = m.transpose(0, 1, 4, 2, 3)
    l = l.transpose(0, 1, 4, 2, 3)
    o = o / l[..., None]
    o = o.reshape(n, lp, h, d)[:, :L]
    m = m.reshape(n, lp, h)[:, :L]
    l = l.reshape(n, lp, h)[:, :L]
    return o, m, l


def dilated_attention(q, k, v):
    b, S, h, d = q.shape
    outs, ms, ls = [], [], []
    for window, dil in B_PATTERNS:
        L = S // dil

        def fold(t):
            return t.reshape(b, L, dil, h, d).transpose(0, 2, 1, 3, 4).reshape(b * dil, L, h, d)

        o, m, l = banded_attention(fold(q), fold(k), fold(v), window // dil)
        outs.append(o.reshape(b, dil, L, h, d).transpose(0, 2, 1, 3, 4).reshape(b, S, h, d))
        ms.append(m.reshape(b, dil, L, h).transpose(0, 2, 1, 3).reshape(b, S, h))
        ls.append(l.reshape(b, dil, L, h).transpose(0, 2, 1, 3).reshape(b, S, h))
    o = jnp.stack(outs)
    m = jnp.stack(ms)
    l = jnp.stack(ls)
    w = l * jnp.exp(m - m.max(axis=0, keepdims=True))
    return (w[..., None] * o).sum(axis=0) / w.sum(axis=0)[..., None]


def setup_inputs(seed: int = 0) -> dict:
    key = jax.random.key(seed)
    ks = jax.random.split(key, 10)
    f32 = jnp.float32
    x = jax.random.normal(ks[0], (BATCH, SEQ, D_MODEL), f32)
    norm_gain = 1.0 + 0.1 * jax.random.normal(ks[1], (DEPTH, D_MODEL), f32)
    w_in = jax.random.normal(ks[2], (DEPTH, D_MODEL, IN_WIDTH), f32) * D_MODEL ** -0.5
    q_norm_a = 1.0 + 0.1 * jax.random.normal(ks[3], (DEPTH, HEAD_DIM), f32)
    k_norm_a = 1.0 + 0.1 * jax.random.normal(ks[4], (DEPTH, HEAD_DIM), f32)
    sinks_a = 0.5 * jax.random.normal(ks[5], (DEPTH, A_HEADS), f32)
    q_norm_b = 1.0 + 0.1 * jax.random.normal(ks[6], (DEPTH, HEAD_DIM), f32)
    k_norm_b = 1.0 + 0.1 * jax.random.normal(ks[7], (DEPTH, HEAD_DIM), f32)
    w_out = jax.random.normal(ks[8], (DEPTH, MIX_WIDTH, D_MODEL), f32) * MIX_WIDTH ** -0.5
    return {"x": x, "norm_gain": norm_gain, "w_in": w_in,
            "q_norm_a": q_norm_a, "k_norm_a": k_norm_a, "sinks_a": sinks_a,
            "q_norm_b": q_norm_b, "k_norm_b": k_norm_b, "w_out": w_out}


def reference(x, norm_gain, w_in, q_norm_a, k_norm_a, sinks_a, q_norm_b, k_norm_b, w_out):
    b, S, _ = x.shape
    pos = jnp.arange(S)
    split_at = [int(c) for c in np.cumsum(IN_SPLITS)[:-1]]
    for i in range(DEPTH):
        hdn = rmsnorm(x, norm_gain[i])
        proj = jnp.einsum("bsd,de->bse", hdn, w_in[i])
        q_a, k_a, v_a, g_a, q_b, k_b, v_b, g_b = jnp.split(proj, split_at, axis=-1)

        q_a = rope(rmsnorm(q_a.reshape(b, S, A_HEADS, HEAD_DIM), q_norm_a[i]), pos)
        k_a = rope(rmsnorm(k_a.reshape(b, S, A_KV_HEADS, HEAD_DIM), k_norm_a[i]), pos)
        v_a = v_a.reshape(b, S, A_KV_HEADS, HEAD_DIM)
        o_a, _, _ = banded_attention(q_a, k_a, v_a, A_WINDOW - 1, sinks=sinks_a[i])
        o_a = o_a.reshape(b, S, A_WIDTH).astype(x.dtype) * jax.nn.silu(g_a)

        q_b = rope(rmsnorm(q_b.reshape(b, S, B_HEADS, HEAD_DIM), q_norm_b[i]), pos)
        k_b = rope(rmsnorm(k_b.reshape(b, S, B_HEADS, HEAD_DIM), k_norm_b[i]), pos)
        v_b = v_b.reshape(b, S, B_HEADS, HEAD_DIM)
        o_b = dilated_attention(q_b, k_b, v_b)
        o_b = o_b.reshape(b, S, B_WIDTH).astype(x.dtype) * jax.nn.silu(g_b)

        mixed = jnp.concatenate([o_a, o_b], axis=-1)
        x = x + jnp.einsum("bse,ed->bsd", mixed, w_out[i])
    return x
```

```python
import math
import numpy as np
import ml_dtypes
import concourse.bass as bass
import concourse.mybir as mybir
from concourse.bass_utils import run_bass_kernel_spmd

F32 = mybir.dt.float32
BF16 = mybir.dt.bfloat16
AF = mybir.ActivationFunctionType
ALU = mybir.AluOpType

S = 4096
D = 1024
E_IN = 3328
NCORES = 8
EPS = 1e-6
TB = 512
NTB = S // TB

C_QA, C_KA, C_VA, C_GA, C_QB, C_KB, C_VB, C_GB = 0, 512, 640, 768, 1280, 1792, 2304, 2816


class Res:
    __slots__ = ("name", "w", "r")

    def __init__(self, name):
        self.name = name
        self.w = {}
        self.r = {}


class Eng:
    def __init__(self, name, sem, is_pe=False, is_queue=False):
        self.name = name
        self.sem = sem
        self.count = 0
        self.ops = []
        self.waited = {}
        self.is_pe = is_pe
        self.is_queue = is_queue


class DmaSem:
    def __init__(self, sem):
        self.sem = sem
        self.count = 0


class Sched:
    def __init__(self, nc, ctx):
        self.nc = nc
        self.ctx = ctx
        mk = lambda n: ctx.enter_context(nc.semaphore(n))
        self.pe = Eng("pe", mk("s_pe"), is_pe=True)
        self.act = Eng("act", mk("s_act"))
        self.dve = Eng("dve", mk("s_dve"))
        self.pool = Eng("pool", mk("s_pool"))
        self.sp = Eng("sp", mk("s_sp"), is_queue=True)
        self.engs = [self.pe, self.act, self.dve, self.pool, self.sp]
        self.dsems = []
        self.dsem_of = {}

    def dma_sem(self, name):
        d = DmaSem(self.ctx.enter_context(self.nc.semaphore(name)))
        self.dsems.append(d)
        self.dsem_of[d.sem] = d
        return d

    def op(self, eng, fn, reads=(), writes=(), dsem=None, partial=False):
        waits = {}

        def need(d):
            for s, v in d.items():
                if waits.get(s, 0) < v:
                    waits[s] = v

        for R in reads:
            need(R.w)
        for W in writes:
            need(W.w)
            need(W.r)
        final = []
        for s, v in waits.items():
            if eng.is_pe and s is eng.sem:
                continue
            if s in self.dsem_of:
                v = self.dsem_of[s].count
            if eng.waited.get(s, 0) >= v:
                continue
            eng.waited[s] = v
            final.append((s, v))
        if dsem is None:
            eng.count += 1
            ev = (eng.sem, eng.count)
            inc = (eng.sem, 1)
        else:
            dsem.count += 16
            ev = (dsem.sem, dsem.count)
            inc = (dsem.sem, 16)
        eng.ops.append((final, fn, inc))
        for R in reads:
            if R.r.get(ev[0], 0) < ev[1]:
                R.r[ev[0]] = ev[1]
        for W in writes:
            if partial:
                W.w[ev[0]] = ev[1]
            else:
                W.w = {ev[0]: ev[1]}
                W.r = {}
        return ev

    def barrier(self):
        allev = {}
        for e in self.engs:
            if not e.is_queue and e.count:
                allev[e.sem] = e.count
        for d in self.dsems:
            if d.count:
                allev[d.sem] = d.count
        for e in self.engs:
            final = []
            for s, v in allev.items():
                if e.waited.get(s, 0) >= v:
                    continue
                e.waited[s] = v
                final.append((s, v))
            if final:
                e.ops.append((final, None, None))

    def flush(self):
        nc = self.nc
        with nc.Block() as block:
            def replay(eng):
                def run(e):
                    for waits, fn, inc in eng.ops:
                        for s, v in waits:
                            e.wait_ge(s, v)
                        if fn is not None:
                            fn(e).then_inc(inc[0], inc[1])
                return run

            block.tensor(replay(self.pe))
            block.scalar(replay(self.act))
            block.vector(replay(self.dve))
            block.gpsimd(replay(self.pool))
            block.sync(replay(self.sp))
        for e in self.engs:
            e.ops = []


def build_nc(debug=False, stop_after=3):
    from contextlib import ExitStack

    nc = bass.Bass("TRN2", target_bir_lowering=False)
    dt = lambda name, shape, dtype, kind: nc.dram_tensor(name, shape, dtype, kind=kind).ap()
    x = dt("x", [S, D], F32, "ExternalInput")
    w_in = dt("w_in", [D, E_IN], F32, "ExternalInput")
    w_out = dt("w_out", [D, D], F32, "ExternalInput")
    gain_b = dt("gain_b", [128, D], F32, "ExternalInput")
    gvec = dt("gvec", [128, 4], F32, "ExternalInput")
    sinks_rep = dt("sinks_rep", [128, 8], F32, "ExternalInput")
    lmask = dt("lmask", [128, 8], F32, "ExternalInput")
    cosT = dt("cosT", [128, S], F32, "ExternalInput")
    sinT = dt("sinT", [128, S], F32, "ExternalInput")
    masks = dt("masks", [5, 128, 1024], BF16, "ExternalInput")
    ident = dt("ident", [128, 128], BF16, "ExternalInput")
    bones = dt("bones", [128, 128], BF16, "ExternalInput")
    rmat = dt("rmat", [128, 128], BF16, "ExternalInput")
    out = dt("out", [S, D], F32, "ExternalOutput")
    sk = "ExternalOutput" if debug else "Internal"
    QTs = dt("QTs", [8, 128, S], BF16, sk)
    KTs = dt("KTs", [5, 128, S], BF16, sk)
    GTs = dt("GTs", [8, 128, S], BF16, sk)
    Vs = dt("Vs", [S, 12, 128], BF16, sk)
    MTs = dt("MTs", [8, 128, S], BF16, sk)

    with ExitStack() as ctx:
        sch = Sched(nc, ctx)
        op = sch.op
        PE, ACT, DVE, POOL, SP = sch.pe, sch.act, sch.dve, sch.pool, sch.sp
        sb = lambda name, shape, dtype, c=ctx: c.enter_context(nc.sbuf_tensor(name, shape, dtype))
        ps = lambda name, shape, dtype, c=ctx: c.enter_context(nc.psum_tensor(name, shape, dtype))

        r_QTs = [Res(f"QTs{i}") for i in range(8)]
        r_KTs = [Res(f"KTs{i}") for i in range(5)]
        r_GTs = [Res(f"GTs{i}") for i in range(8)]
        r_Vs = Res("Vs")
        r_MTs = [Res(f"MTs{i}") for i in range(8)]

        ident_sb = sb("ident_sb", [128, 128], BF16)
        r_ident = Res("ident")
        wo16 = sb("wo16", [128, 8, D], BF16)
        r_wo = Res("wo")
        d_wo = sch.dma_sem("d_wo")
        d_const = sch.dma_sem("d_const")
        op(SP, lambda e: e.dma_start(out=ident_sb[:], in_=ident), writes=[r_ident], dsem=d_const)

        with ExitStack() as c1:
            sb1 = lambda n, s, d: sb(n, s, d, c1)
            ps1 = lambda n, s, d: ps(n, s, d, c1)
            w16 = sb1("w16", [128, 8, E_IN], BF16)
            gainb = sb1("gainb", [128, D], F32)
            gv = sb1("gv", [128, 4], F32)
            bones_sb = sb1("bones_sb", [128, 128], BF16)
            rmat_sb = sb1("rmat_sb", [128, 128], BF16)
            NXT = 4
            xt = [sb1(f"xt{i}", [128, D], F32) for i in range(NXT)]
            xs16 = [sb1(f"xs16_{i}", [128, D], BF16) for i in range(4)]
            junk16 = sb1("junk16", [128, D], BF16)
            ss = sb1("ss", [128, 64], F32)
            lnss = sb1("lnss", [128, 64], F32)
            rstd = sb1("rstd", [128, 64], F32)
            xsT = [sb1(f"xsT{i}", [128, 8, TB], BF16) for i in range(2)]
            cosb = [sb1(f"cosb{i}", [128, TB], F32) for i in range(2)]
            sinb = [sb1(f"sinb{i}", [128, TB], F32) for i in range(2)]
            NCH = 4
            sq16 = [sb1(f"sq16_{i}", [128, TB], BF16) for i in range(NCH)]
            qc16 = [sb1(f"qc16_{i}", [128, TB], BF16) for i in range(NCH)]
            lnv = [sb1(f"lnv{i}", [128, TB], F32) for i in range(NCH)]
            rinv = [sb1(f"rinv{i}", [128, TB], F32) for i in range(NCH)]
            ta = [sb1(f"ta{i}", [128, TB], F32) for i in range(NCH)]
            tb_ = [sb1(f"tb{i}", [128, TB], F32) for i in range(NCH)]
            tc = [sb1(f"tc{i}", [128, TB], F32) for i in range(NCH)]
            NST = 4
            o16 = [sb1(f"o16_{i}", [128, TB], BF16) for i in range(NST)]
            g16 = [sb1(f"g16_{i}", [128, TB], BF16) for i in range(NST)]
            vst = [sb1(f"vst{i}", [128, 12, 128], BF16) for i in range(2)]

            tr_ps = ps1("tr_ps", [128, 8, 128], BF16)
            NPJ = 3
            pj_ps = [ps1(f"pj_ps{i}", [128, TB], F32) for i in range(NPJ)]
            ms_ps = [ps1(f"ms_ps{i}", [128, TB], F32) for i in range(2)]
            rot_ps = [ps1(f"rot_ps{i}", [128, TB], F32) for i in range(2)]

            R = lambda n: Res(n)
            r_w16 = [R(f"w16_{i}") for i in range(10)]
            r_gainb, r_gv, r_bones, r_rmat = R("gainb"), R("gv"), R("bones"), R("rmat")
            r_xt = [R(f"xt{i}") for i in range(NXT)]
            r_xs16 = [R(f"xs16{i}") for i in range(4)]
            r_junk = R("junk")
            r_ss, r_lnss, r_rstd = R("ss"), R("lnss"), R("rstd")
            r_xsT = [R(f"xsT{i}") for i in range(2)]
            r_cos = [R(f"cos{i}") for i in range(2)]
            r_sin = [R(f"sin{i}") for i in range(2)]
            r_sq = [R(f"sq{i}") for i in range(NCH)]
            r_qc = [R(f"qc{i}") for i in range(NCH)]
            r_lnv = [R(f"lnv{i}") for i in range(NCH)]
            r_rinv = [R(f"rinv{i}") for i in range(NCH)]
            r_ta = [R(f"ta{i}") for i in range(NCH)]
            r_tb = [R(f"tb{i}") for i in range(NCH)]
            r_tc = [R(f"tc{i}") for i in range(NCH)]
            r_o16 = [R(f"o16{i}") for i in range(NST)]
            r_g16 = [R(f"g16{i}") for i in range(NST)]
            r_vst = [R(f"vst{i}") for i in range(2)]
            r_tr = R("tr_ps")
            r_pj = [R(f"pj{i}") for i in range(NPJ)]
            r_ms = [R(f"ms{i}") for i in range(2)]
            r_rot = [R(f"rot{i}") for i in range(2)]

            d_w = [sch.dma_sem(f"d_w{i}") for i in range(10)]
            d_xt = [sch.dma_sem(f"d_xt{i}") for i in range(NXT)]
            d_cs = [sch.dma_sem(f"d_cs{i}") for i in range(2)]
            d_o16 = [sch.dma_sem(f"d_o16{i}") for i in range(NST)]
            d_g16 = [sch.dma_sem(f"d_g16{i}") for i in range(NST)]
            d_vst = [sch.dma_sem(f"d_vst{i}") for i in range(2)]

            op(SP, lambda e: e.dma_start(out=gainb[:], in_=gain_b), writes=[r_gainb], dsem=d_const)
            op(SP, lambda e: e.dma_start(out=gv[:], in_=gvec), writes=[r_gv], dsem=d_const)
            op(SP, lambda e: e.dma_start(out=bones_sb[:], in_=bones), writes=[r_bones], dsem=d_const)
            op(SP, lambda e: e.dma_start(out=rmat_sb[:], in_=rmat), writes=[r_rmat], dsem=d_const)
            for i in range(2):
                op(POOL, lambda e, i=i: e.memset(vst[i][:], 1.0), writes=[r_vst[i]])
            w_view = w_in.rearrange("(c p) e -> p c e", p=128)
            wblocks = [(0, 128), (128, 256), (256, 384), (384, 512), (512, 1024), (1280, 1792), (1792, 2304),
                       (2304, 2816), (1024, 1280), (2816, 3328)]
            wblk_of_col = {}
            for bi, (c0, c1_) in enumerate(wblocks):
                for cc in range(c0, c1_, 128):
                    wblk_of_col[cc] = bi
                op(POOL, lambda e, c0=c0, c1_=c1_: e.dma_start(out=w16[:, :, c0:c1_], in_=w_view[:, :, c0:c1_]),
                   writes=[r_w16[bi]], dsem=d_w[bi])
            op(POOL, lambda e: e.dma_start(out=wo16[:], in_=w_out.rearrange("(c p) e -> p c e", p=128)),
               writes=[r_wo], dsem=d_wo)


            def xload(n, j):
                bs = n % 2
                k = n * 4 + j
                sl = k % NXT
                row0 = n * TB + j * 128
                op(SP, lambda e: e.dma_start(out=xt[sl][:], in_=x[row0:row0 + 128, :]),
                   writes=[r_xt[sl]], dsem=d_xt[sl])
                if j == 0:
                    op(SP, lambda e: e.dma_start(out=cosb[bs][:], in_=cosT[:, n * TB:(n + 1) * TB]),
                       writes=[r_cos[bs]], dsem=d_cs[bs])
                    op(SP, lambda e: e.dma_start(out=sinb[bs][:], in_=sinT[:, n * TB:(n + 1) * TB]),
                       writes=[r_sin[bs]], dsem=d_cs[bs])

            def xprep1(n, j):
                bs = n % 2
                k = n * 4 + j
                sl = k % NXT
                s2 = k % 4
                col = k % 64
                op(ACT, lambda e: e.activation(out=junk16[:], in_=xt[sl][:], func=AF.Square,
                                               accum_out=ss[:, col:col + 1]),
                   reads=[r_xt[sl]], writes=[r_junk, r_ss], partial=True)
                op(ACT, lambda e: e.activation(out=lnss[:, col:col + 1], in_=ss[:, col:col + 1], func=AF.Ln,
                                               scale=1.0 / D, bias=EPS),
                   reads=[r_ss], writes=[r_lnss], partial=True)
                op(ACT, lambda e: e.activation(out=rstd[:, col:col + 1], in_=lnss[:, col:col + 1], func=AF.Exp,
                                               scale=-0.5),
                   reads=[r_lnss], writes=[r_rstd], partial=True)
                op(DVE, lambda e: e.scalar_tensor_tensor(
                    out=xs16[s2][:], in0=xt[sl][:], scalar=rstd[:, col:col + 1], in1=gainb[:],
                    op0=ALU.mult, op1=ALU.mult),
                   reads=[r_xt[sl], r_rstd, r_gainb], writes=[r_xs16[s2]])

            def xprep2(n, j):
                bs = n % 2
                s2 = (n * 4 + j) % 4
                for c in range(8):
                    op(PE, lambda e, c=c: e.transpose(out=tr_ps[:, c, :], in_=xs16[s2][:, c * 128:(c + 1) * 128],
                                                      identity=ident_sb[:]),
                       reads=[r_xs16[s2], r_ident], writes=[r_tr], partial=(c > 0))
                op(ACT, lambda e: e.activation(out=xsT[bs][:, :, j * 128:(j + 1) * 128], in_=tr_ps[:], func=AF.Copy),
                   reads=[r_tr], writes=[r_xsT[bs]], partial=(j > 0))

            pj_ctr = [0]
            ch_ctr = [0]
            st_ctr = [0]
            gst_ctr = [0]
            vst_ctr = [0]

            def proj_fm(n, col0):
                bs = n % 2
                k = pj_ctr[0]
                pj_ctr[0] += 1
                pslot = k % NPJ
                for dc in range(8):
                    op(PE, lambda e, pslot=pslot, dc=dc, bs=bs: e.matmul(
                        pj_ps[pslot][:], lhsT=w16[:, dc, col0:col0 + 128], rhs=xsT[bs][:, dc, :],
                        start=(dc == 0), stop=(dc == 7)),
                       reads=[r_w16[wblk_of_col[col0]], r_xsT[bs]], writes=[r_pj[pslot]], partial=(dc > 0))
                return pslot

            def qk_chain_a(n, pslot, gcol):
                k = ch_ctr[0]
                ch_ctr[0] += 1
                cs = k % NCH
                m2 = k % 2
                op(ACT, lambda e: e.activation(out=sq16[cs][:], in_=pj_ps[pslot][:], func=AF.Square),
                   reads=[r_pj[pslot]], writes=[r_sq[cs]])
                op(ACT, lambda e: e.activation(out=qc16[cs][:], in_=pj_ps[pslot][:], func=AF.Identity,
                                               scale=gv[:, gcol:gcol + 1]),
                   reads=[r_pj[pslot], r_gv], writes=[r_qc[cs]])
                return (k, cs, m2)

            def qk_chain_pe(state):
                k, cs, m2 = state
                op(PE, lambda e: e.matmul(ms_ps[m2][:], lhsT=bones_sb[:], rhs=sq16[cs][:], start=True, stop=True),
                   reads=[r_bones, r_sq[cs]], writes=[r_ms[m2]])
                op(PE, lambda e: e.matmul(rot_ps[m2][:], lhsT=rmat_sb[:], rhs=qc16[cs][:], start=True, stop=True),
                   reads=[r_rmat, r_qc[cs]], writes=[r_rot[m2]])

            def qk_chain_b1(n, pslot, gcol, state):
                k, cs, m2 = state
                bs = n % 2
                op(DVE, lambda e: e.scalar_tensor_tensor(out=ta[cs][:], in0=pj_ps[pslot][:], scalar=gv[:, gcol:gcol + 1],
                                                         in1=cosb[bs][:], op0=ALU.mult, op1=ALU.mult),
                   reads=[r_pj[pslot], r_gv, r_cos[bs], r_qc[cs]], writes=[r_ta[cs]])
                op(DVE, lambda e: e.tensor_tensor(out=tb_[cs][:], in0=rot_ps[m2][:], in1=sinb[bs][:], op=ALU.mult),
                   reads=[r_rot[m2], r_sin[bs]], writes=[r_tb[cs]])
                op(POOL, lambda e: e.tensor_tensor(out=tc[cs][:], in0=ta[cs][:], in1=tb_[cs][:], op=ALU.add),
                   reads=[r_ta[cs], r_tb[cs]], writes=[r_tc[cs]])

            def qk_chain_b2(n, is_q, state, dst, r_dst):
                k, cs, m2 = state
                op(ACT, lambda e: e.activation(out=lnv[cs][:], in_=ms_ps[m2][:], func=AF.Ln, bias=EPS),
                   reads=[r_ms[m2]], writes=[r_lnv[cs]])
                op(ACT, lambda e: e.activation(out=rinv[cs][:], in_=lnv[cs][:], func=AF.Exp, scale=-0.5,
                                               bias=(-math.log(8.0) if is_q else 0.0)),
                   reads=[r_lnv[cs]], writes=[r_rinv[cs]])
                s = st_ctr[0] % NST
                st_ctr[0] += 1
                op(DVE, lambda e: e.tensor_tensor(out=o16[s][:], in0=tc[cs][:], in1=rinv[cs][:], op=ALU.mult),
                   reads=[r_tc[cs], r_rinv[cs]], writes=[r_o16[s]])
                op(SP, lambda e: e.dma_start(out=dst[:, n * TB:(n + 1) * TB], in_=o16[s][:]),
                   reads=[r_o16[s]], writes=[r_dst], dsem=d_o16[s], partial=True)

            def gate_chunk(n, pslot, dst, r_dst):
                s = gst_ctr[0] % NST
                gst_ctr[0] += 1
                op(ACT, lambda e: e.activation(out=g16[s][:], in_=pj_ps[pslot][:], func=AF.Silu),
                   reads=[r_pj[pslot]], writes=[r_g16[s]])
                op(SP, lambda e: e.dma_start(out=dst[:, n * TB:(n + 1) * TB], in_=g16[s][:]),
                   reads=[r_g16[s]], writes=[r_dst], dsem=d_g16[s], partial=True)

            def v_tile(n, j):
                bs = n % 2
                k = pj_ctr[0]
                pj_ctr[0] += 2
                pa, pb = k % NPJ, (k + 1) % NPJ
                for dc in range(8):
                    op(PE, lambda e, dc=dc: e.matmul(pj_ps[pb][:], lhsT=xsT[bs][:, dc, j * 128:(j + 1) * 128],
                                                     rhs=w16[:, dc, C_VB:C_VB + 512], start=(dc == 0), stop=(dc == 7)),
                       reads=[r_w16[wblk_of_col[C_VB]], r_xsT[bs]], writes=[r_pj[pb]], partial=(dc > 0))
                for dc in range(8):
                    op(PE, lambda e, dc=dc: e.matmul(pj_ps[pa][:, 0:128], lhsT=xsT[bs][:, dc, j * 128:(j + 1) * 128],
                                                     rhs=w16[:, dc, C_VA:C_VA + 128], start=(dc == 0), stop=(dc == 7)),
                       reads=[r_w16[wblk_of_col[C_VA]], r_xsT[bs]], writes=[r_pj[pa]], partial=(dc > 0))
                vs = vst_ctr[0] % 2
                vst_ctr[0] += 1
                vv = vst[vs][:].rearrange("p (m two) (h e) -> p m (two h) e", two=2, h=2)
                op(DVE, lambda e: e.tensor_copy(out=vv[:, 2:6, 0:4:3, :],
                                                in_=pj_ps[pb][:].rearrange("p (m two e) -> p m two e", two=2, e=64)),
                   reads=[r_pj[pb]], writes=[r_vst[vs]])
                va_in = pj_ps[pa][:, 0:128].rearrange("p (k e) -> p k e", e=64)
                op(DVE, lambda e: e.tensor_copy(out=vv[:, 0:2, 0, :], in_=va_in),
                   reads=[r_pj[pa]], writes=[r_vst[vs]], partial=True)
                op(DVE, lambda e: e.tensor_copy(out=vv[:, 0:2, 3, :], in_=va_in),
                   reads=[r_pj[pa]], writes=[r_vst[vs]], partial=True)
                row0 = n * TB + j * 128
                op(SP, lambda e: e.dma_start(out=Vs[row0:row0 + 128, :, :], in_=vst[vs][:]),
                   reads=[r_vst[vs]], writes=[r_Vs], dsem=d_vst[vs], partial=True)

            chunks = []
            for i in range(4):
                chunks.append(("q", C_QA + 128 * i, 0, QTs[i], r_QTs[i]))
            chunks.append(("k", C_KA, 1, KTs[0], r_KTs[0]))
            for i in range(4):
                chunks.append(("q", C_QB + 128 * i, 2, QTs[4 + i], r_QTs[4 + i]))
            for i in range(4):
                chunks.append(("k", C_KB + 128 * i, 3, KTs[1 + i], r_KTs[1 + i]))
            gchunks = []
            for i in range(4):
                gchunks.append((C_GA + 128 * i, GTs[i], r_GTs[i]))
            for i in range(4):
                gchunks.append((C_GB + 128 * i, GTs[4 + i], r_GTs[4 + i]))

            for j in range(4):
                xload(0, j)
            for j in range(4):
                xprep1(0, j)
            for j in range(4):
                xprep2(0, j)
            NQK = len(chunks)
            for n in range(NTB):
                st8 = {}

                def s1(i):
                    kind, col0, gcol, dst, r_dst = chunks[i]
                    pslot = proj_fm(n, col0)
                    st8[i] = (pslot, qk_chain_a(n, pslot, gcol))

                def s2(i):
                    pslot, state = st8[i]
                    qk_chain_pe(state)
                    qk_chain_b1(n, pslot, chunks[i][2], state)

                def s3(i):
                    pslot, state = st8[i]
                    qk_chain_b2(n, chunks[i][0] == "q", state, chunks[i][3], chunks[i][4])

                if n + 1 < NTB:
                    for j in range(4):
                        xload(n + 1, j)
                for i in range(NQK):
                    s1(i)
                    if i >= 1:
                        s2(i - 1)
                    if i >= 2:
                        s3(i - 2)
                    if n + 1 < NTB and i in (1, 4, 7, 10):
                        xprep1(n + 1, (i - 1) // 3)
                v_tile(n, 0)
                s2(NQK - 1)
                v_tile(n, 1)
                s3(NQK - 2)
                v_tile(n, 2)
                s3(NQK - 1)
                v_tile(n, 3)
                for gi, (col0, dst, r_dst) in enumerate(gchunks):
                    pslot = proj_fm(n, col0)
                    gate_chunk(n, pslot, dst, r_dst)
                    if n + 1 < NTB and gi < 4:
                        xprep2(n + 1, gi)
            sch.barrier()
            sch.flush()

        if stop_after >= 2:
            with ExitStack() as c2:
                sb2 = lambda n, s, d: sb(n, s, d, c2)
                ps2 = lambda n, s, d: ps(n, s, d, c2)
                mask_sb = sb2("mask_sb", [128, 5, 1024], BF16)
                sinkb = sb2("sinkb", [128, 8], F32)
                lmask_sb = sb2("lmask_sb", [128, 8], F32)
                QTc = [sb2(f"QTc{i}", [128, S], BF16) for i in range(2)]
                KTc = [sb2(f"KTc{i}", [128, S], BF16) for i in range(2)]
                Gc = [sb2(f"Gc{i}", [128, S], BF16) for i in range(2)]
                NV = 4
                Vt = [sb2(f"Vt{i}", [128, 32, 256], BF16) for i in range(NV)]
                acc = [sb2(f"acc{i}", [128, S], F32) for i in range(2)]
                Rt = sb2("Rt", [128, 4, 512], F32)
                Mx = sb2("Mx", [128, 4, 512], BF16)
                QTf = {4: sb2("QTf4", [128, S], BF16), 16: sb2("QTf16", [128, S], BF16)}
                r_QTf = {4: Res("QTf4"), 16: Res("QTf16")}
                NP = 4
                Pb = [sb2(f"Pb{i}", [128, 1024], BF16) for i in range(NP)]
                NSTB = 2
                st_ps = [ps2(f"st_ps{i}", [128, 1024], F32) for i in range(NSTB)]
                ot_ps = [ps2(f"ot_ps{i}", [128, 512], F32) for i in range(2)]
                dm_ps = ps2("dm_ps", [128, 512], F32)
                r_dm = Res("dm")
                NFILL = 1

                R = lambda n: Res(n)
                r_mask, r_sinkb, r_lmask = R("mask"), R("sinkb"), R("lmask")
                r_QTc = [R(f"QTc{i}") for i in range(2)]
                r_KTc = [R(f"KTc{i}") for i in range(2)]
                r_Gc = [R(f"Gc{i}") for i in range(2)]
                r_Vt = [R(f"Vt{i}") for i in range(NV)]
                r_acc = [R(f"acc{i}") for i in range(2)]
                r_Rt = R("Rt")
                r_Mx = [R(f"Mx{i}") for i in range(4)]
                r_Pb = [R(f"Pb{i}") for i in range(NP)]
                r_st = [R(f"st{i}") for i in range(NSTB)]
                r_ot = [R(f"ot{i}") for i in range(2)]
                d_c2 = sch.dma_sem("d_c2")
                d_q = [sch.dma_sem(f"d_q{i}") for i in range(2)]
                d_k = [sch.dma_sem(f"d_k{i}") for i in range(2)]
                d_g = [sch.dma_sem(f"d_g{i}") for i in range(2)]
                d_v = [sch.dma_sem(f"d_v{i}") for i in range(NV)]
                d_mx = [sch.dma_sem(f"d_mx{i}") for i in range(4)]

                op(SP, lambda e: e.dma_start(out=mask_sb[:], in_=masks.rearrange("m p f -> p m f")),
                   writes=[r_mask], dsem=d_c2)
                op(SP, lambda e: e.dma_start(out=sinkb[:], in_=sinks_rep), writes=[r_sinkb], dsem=d_c2)
                op(SP, lambda e: e.dma_start(out=lmask_sb[:], in_=lmask), writes=[r_lmask], dsem=d_c2)
                op(ACT, lambda e: e.activation(out=sinkb[:], in_=sinkb[:], func=AF.Exp), reads=[], writes=[r_sinkb])
                op(DVE, lambda e: e.tensor_tensor(out=sinkb[:], in0=sinkb[:], in1=lmask_sb[:], op=ALU.mult),
                   reads=[r_lmask], writes=[r_sinkb])

                Vs_flat = Vs.rearrange("t s e -> t (s e)")
                bt_ctr = [0]

                def load_pair(c):
                    sl = c % 2
                    op(SP, lambda e: e.dma_start(out=QTc[sl][:], in_=QTs[c]), reads=[r_QTs[c]], writes=[r_QTc[sl]],
                       dsem=d_q[sl])
                    if c < 4:
                        kv = c // 2
                        op(SP, lambda e: e.dma_start(out=KTc[sl][0:64, :], in_=KTs[0][64 * kv:64 * kv + 64, :]),
                           reads=[r_KTs[0]], writes=[r_KTc[sl]], dsem=d_k[sl])
                        op(SP, lambda e: e.dma_start(out=KTc[sl][64:128, :], in_=KTs[0][64 * kv:64 * kv + 64, :]),
                           reads=[r_KTs[0]], writes=[r_KTc[sl]], dsem=d_k[sl], partial=True)
                    else:
                        op(SP, lambda e: e.dma_start(out=KTc[sl][:], in_=KTs[1 + (c - 4)]), reads=[r_KTs[1 + (c - 4)]],
                           writes=[r_KTc[sl]], dsem=d_k[sl])

                def load_G(c):
                    sl = c % 2
                    op(SP, lambda e: e.dma_start(out=Gc[sl][:], in_=GTs[c]), reads=[r_GTs[c]], writes=[r_Gc[sl]],
                       dsem=d_g[sl])

                def load_v(c, dil):
                    vs_ = {1: (0 if c % 2 == 0 else 3), 4: 1, 16: 2}[dil]
                    slot0 = 2 * (c // 2) if c < 4 else 4 + 2 * (c - 4)
                    f0 = slot0 * 128
                    nb = 32 // dil
                    first = True
                    for r in range(dil):
                        src = Vs_flat.rearrange("(b i r) f -> r i b f", i=128, r=dil)[r][:, :, f0:f0 + 256]
                        op(SP, lambda e, src=src, r=r: e.dma_start(out=Vt[vs_][:, r * nb:(r + 1) * nb, :], in_=src),
                           reads=[r_Vs], writes=[r_Vt[vs_]], dsem=d_v[vs_], partial=not first)
                        first = False
                    return vs_

                NBLK = S // 512
                r_accb = [[Res(f"acc{h}_{j}") for j in range(NBLK)] for h in range(2)]

                def make_job(c, sl, hl, dil, vs_, is_a, first_pat, bt):
                    nb = 32 // dil
                    k = bt_ctr[0]
                    bt_ctr[0] += 1
                    units = [4 * bt + i for i in range(4)]
                    firsts = [(u % nb) == 0 for u in units]
                    if is_a:
                        mi = 4 if firsts[0] else 3
                    elif firsts[0] and firsts[2]:
                        mi = 2
                    elif firsts[0]:
                        mi = 1
                    else:
                        mi = 0
                    return dict(c=c, sl=sl, hl=hl, dil=dil, vs_=vs_, is_a=is_a, first_pat=first_pat, bt=bt, nb=nb,
                                p0=64 * hl, stb=k % NSTB, pb=k % NP, ob=k % 2, units=units, firsts=firsts, mi=mi)

                def stage_a(J):
                    sl, dil, nb, p0, stb, pb, mi = J["sl"], J["dil"], J["nb"], J["p0"], J["stb"], J["pb"], J["mi"]

                    def tok(u):
                        r, b = u // nb, u % nb
                        start = r + dil * 128 * b
                        return slice(start, start + dil * 127 + 1, dil)

                    units, firsts = J["units"], J["firsts"]

                    def tok2(u):
                        r, b = u // nb, u % nb
                        start = r + dil * 128 * b
                        return slice(start, start + dil * 255 + 1, dil)

                    def qk(kunit, qunit, pos, n, first):
                        if dil == 1:
                            start = 128 * qunit
                            rhs = QTc[sl][p0:p0 + 64, start:start + n]
                            rq = r_QTc[sl]
                        else:
                            r, b = qunit // nb, qunit % nb
                            start = r * (S // dil) + 128 * b
                            rhs = QTf[dil][p0:p0 + 64, start:start + n]
                            rq = r_QTf[dil]
                        op(PE, lambda e: e.matmul(
                            st_ps[stb][:, 128 * pos:128 * pos + n], lhsT=KTc[sl][p0:p0 + 64, tok(kunit)],
                            rhs=rhs, start=True, stop=True),
                           reads=[r_KTc[sl], rq], writes=[r_st[stb]], partial=not first)

                    u0 = units[0]
                    qk(u0 if u0 == 0 else u0 - 1, u0, 7, 128, True)
                    for i in range(3):
                        if not firsts[i + 1]:
                            qk(units[i], units[i], 2 * i, 256, False)
                        else:
                            qk(units[i], units[i], 2 * i, 128, False)
                            qk(units[i], units[i + 1], 2 * i + 1, 128, False)
                    qk(units[3], units[3], 6, 128, False)
                    for _ in range(NFILL):
                        op(PE, lambda e: e.matmul(dm_ps[:], lhsT=mask_sb[:, 0, 0:128], rhs=mask_sb[:, 1, 0:512],
                                                  start=True, stop=True),
                           reads=[r_mask], writes=[r_dm])
                    op(ACT, lambda e: e.activation(out=Pb[pb][:], in_=st_ps[stb][:], func=AF.Exp),
                       reads=[r_st[stb]], writes=[r_Pb[pb]])
                    op(DVE, lambda e: e.tensor_tensor(out=Pb[pb][:], in0=Pb[pb][:], in1=mask_sb[:, mi, :], op=ALU.mult),
                       reads=[r_mask], writes=[r_Pb[pb]])

                def stage_b(J):
                    c, hl, dil, nb, pb, ob, vs_ = J["c"], J["hl"], J["dil"], J["nb"], J["pb"], J["ob"], J["vs_"]
                    units = J["units"]
                    a_t = acc[hl]
                    def pv(vunit, pos, col, n, start, stop, first):
                        op(PE, lambda e: e.matmul(
                            ot_ps[ob][:, col:col + n], lhsT=Vt[vs_][:, vunit, 128 * hl:128 * hl + 128],
                            rhs=Pb[pb][:, 128 * pos:128 * pos + n], start=start, stop=stop, skip_group_check=True),
                           reads=[r_Vt[vs_], r_Pb[pb]], writes=[r_ot[ob]], partial=not first)

                    u0 = units[0]
                    pv(u0 if u0 == 0 else u0 - 1, 7, 0, 128, True, False, True)
                    for i in range(3):
                        pv(units[i], 2 * i, 128 * i, 256, False, False, False)
                    pv(units[3], 6, 384, 128, False, True, False)
                    if dil == 16:
                        r0 = units[0] // nb
                        dst = a_t[:].rearrange("p (j r) -> p r j", r=16)[:, r0:r0 + 2, :]
                        src = ot_ps[ob][:].rearrange("p (r j) -> p r j", r=2)
                        blks = list(range(NBLK))
                    else:
                        r, b0 = units[0] // nb, units[0] % nb
                        start = r + dil * 128 * b0
                        dst = a_t[:, start:start + dil * 511 + 1:dil]
                        src = ot_ps[ob][:]
                        blks = list(range(start // 512, (start + dil * 511) // 512 + 1))
                    wr = [r_accb[hl][j] for j in blks]
                    if J["first_pat"]:
                        if J["is_a"]:
                            h = 2 * c + hl
                            op(DVE, lambda e: e.tensor_scalar(out=dst, in0=src, scalar1=sinkb[:, h:h + 1], scalar2=None,
                                                              op0=ALU.add),
                               reads=[r_ot[ob], r_sinkb], writes=wr, partial=True)
                        else:
                            op(DVE, lambda e: e.tensor_copy(out=dst, in_=src),
                               reads=[r_ot[ob]], writes=wr, partial=True)
                    else:
                        op(DVE, lambda e: e.tensor_tensor(out=dst, in0=src, in1=dst, op=ALU.add),
                           reads=[r_ot[ob]], writes=wr, partial=True)

                def fin_act(c, sl, j):
                    cs_ = slice(512 * j, 512 * j + 512)
                    jb = j % 4
                    rR = r_Rtb[jb]
                    op(ACT, lambda e: e.activation(out=Rt[0:64, jb, :], in_=acc[0][64:128, cs_], func=AF.Ln),
                       reads=[r_accb[0][j]], writes=[rR])
                    op(ACT, lambda e: e.activation(out=Rt[64:128, jb, :], in_=acc[1][0:64, cs_], func=AF.Ln),
                       reads=[r_accb[1][j]], writes=[rR], partial=True)
                    op(ACT, lambda e: e.activation(out=Rt[:, jb, :], in_=Rt[:, jb, :], func=AF.Exp, scale=-1.0),
                       reads=[], writes=[rR])

                def fin_pool(c, sl, j):
                    cs_ = slice(512 * j, 512 * j + 512)
                    jb = j % 4
                    op(POOL, lambda e: e.tensor_tensor(out=Rt[:, jb, :], in0=Rt[:, jb, :], in1=Gc[sl][:, cs_], op=ALU.mult),
                       reads=[r_Gc[sl]], writes=[r_Rtb[jb]])

                def fin_dve(c, sl, j):
                    cs_ = slice(512 * j, 512 * j + 512)
                    jb = j % 4
                    rR = r_Rtb[jb]
                    op(DVE, lambda e: e.tensor_tensor(out=Mx[0:64, jb, :], in0=acc[0][0:64, cs_], in1=Rt[0:64, jb, :],
                                                      op=ALU.mult),
                       reads=[r_accb[0][j], rR], writes=[r_Mx[jb]])
                    op(DVE, lambda e: e.tensor_tensor(out=Mx[64:128, jb, :], in0=acc[1][64:128, cs_],
                                                      in1=Rt[64:128, jb, :], op=ALU.mult),
                       reads=[r_accb[1][j], rR], writes=[r_Mx[jb]], partial=True)

                def fin_store(c, sl, j):
                    cs_ = slice(512 * j, 512 * j + 512)
                    jb = j % 4
                    op(ACT, lambda e: e.dma_start(out=MTs[c][:, cs_], in_=Mx[:, jb, :]),
                       reads=[r_Mx[jb]], writes=[r_MTs[c]], dsem=d_mx[jb], partial=(j > 0))

                FIN_STAGES = [fin_act, fin_pool, fin_dve, fin_store]

                act_bg = []

                def deinterleave(c, dil, spread=False):
                    sl = c % 2
                    L = S // dil
                    nparts = 4
                    rr = dil // nparts
                    if dil == 16 and spread:
                        src = QTf[4][:].rearrange("p (r i a) -> p a r i", r=4, a=4)
                        dst = QTf[16][:].rearrange("p (a r i) -> p a r i", a=4, r=4)
                        rsrc = r_QTf[4]

                        def piece(q):
                            op(ACT, lambda e: e.activation(out=dst[:, q, :, :], in_=src[:, q, :, :], func=AF.Copy),
                               reads=[rsrc], writes=[r_QTf[dil]], partial=(q > 0))
                    else:
                        src = QTc[sl][:].rearrange("p (i r) -> p r i", r=dil)
                        dst = QTf[dil][:].rearrange("p (r i) -> p r i", r=dil)

                        def piece(q):
                            op(ACT, lambda e: e.activation(out=dst[:, q * rr:(q + 1) * rr, :],
                                                           in_=src[:, q * rr:(q + 1) * rr, :], func=AF.Copy),
                               reads=[r_QTc[sl]], writes=[r_QTf[dil]], partial=(q > 0))

                    for q in range(nparts):
                        if spread:
                            act_bg.append(((c, dil), lambda q=q: piece(q)))
                        else:
                            piece(q)

                r_Rtb = [Res(f"Rt{i}") for i in range(4)]

                def dils_of(c):
                    return [1] if c < 4 else [1, 4, 16]

                load_pair(0)
                load_G(0)
                load_G(1)
                vslot = {}
                for d in dils_of(0):
                    vslot[(0, d)] = load_v(0, d)
                seq = []
                for c in range(8):
                    sl = c % 2
                    is_a = c < 4
                    dils = dils_of(c)
                    if c + 1 < 8:
                        def pre(c=c):
                            load_pair(c + 1)
                            vslot[(c + 1, 1)] = load_v(c + 1, 1)
                        seq.append(("pre", pre))
                    for hl in range(2):
                        for pi, dil in enumerate(dils):
                            for bt in range(8):
                                seq.append(("job", (c, sl, hl, dil, is_a, pi == 0, bt)))
                            if hl == 1 and dil > 1 and c + 1 < 8:
                                def post_v(c=c, dil=dil):
                                    vslot[(c + 1, dil)] = load_v(c + 1, dil)
                                    deinterleave(c + 1, dil, spread=True)
                                seq.append(("post_deferred", post_v))
                    if c == 3:
                        def post3():
                            for d in (4, 16):
                                vslot[(4, d)] = load_v(4, d)
                                deinterleave(4, d)
                        seq.append(("post", post3))
                    seq.append(("fin", (c, sl)))
                LAG = 2
                pending = []

                fin_state = {"c": None, "sl": None, "t": 0}

                def fin_active():
                    return fin_state["c"] is not None

                def fin_step():
                    c_, sl_, t = fin_state["c"], fin_state["sl"], fin_state["t"]
                    for st_i in (3, 2, 1, 0):
                        j_ = t - st_i
                        if 0 <= j_ < NBLK:
                            FIN_STAGES[st_i](c_, sl_, j_)
                    fin_state["t"] = t + 1
                    if t + 1 >= NBLK + 3:
                        fin_state["c"] = None
                        if c_ + 2 < 8:
                            load_G(c_ + 2)

                deferred = []
                jcount = [0]

                def do_stage_b(J):
                    stage_b(J)
                    while deferred and deferred[0][0] <= J["idx"]:
                        deferred.pop(0)[1]()

                def drain():
                    while pending:
                        do_stage_b(pending.pop(0))
                    while deferred:
                        deferred.pop(0)[1]()

                for kind, item in seq:
                    if kind == "pre":
                        item()
                    elif kind == "job":
                        c, sl, hl, dil, is_a, fp, bt = item
                        while any(tag == (c, dil) for tag, _ in act_bg):
                            act_bg.pop(0)[1]()
                        J = make_job(c, sl, hl, dil, vslot[(c, dil)], is_a, fp, bt)
                        J["idx"] = jcount[0]
                        jcount[0] += 1
                        stage_a(J)
                        pending.append(J)
                        if act_bg and not fin_active():
                            act_bg.pop(0)[1]()
                        if fin_active():
                            fin_step()
                        if len(pending) > LAG:
                            do_stage_b(pending.pop(0))
                    elif kind == "post_deferred":
                        deferred.append((jcount[0] - 1, item))
                    elif kind == "post":
                        drain()
                        item()
                    else:
                        drain()
                        while fin_active():
                            fin_step()
                        c, sl = item
                        fin_state.update(c=c, sl=sl, t=0)
                drain()
                while act_bg:
                    act_bg.pop(0)[1]()
                while fin_active():
                    fin_step()
                sch.barrier()
                sch.flush()

        if stop_after >= 3:
            with ExitStack() as c3:
                sb3 = lambda n, s, d: sb(n, s, d, c3)
                ps3 = lambda n, s, d: ps(n, s, d, c3)
                mt = [sb3(f"mt{i}", [128, 8, TB], BF16) for i in range(2)]
                NX3 = 4
                x3 = [sb3(f"x3_{i}", [128, D], F32) for i in range(NX3)]
                o3 = [sb3(f"o3_{i}", [128, D], F32) for i in range(NX3)]
                op_ps = [ps3(f"op_ps{i}", [128, D], F32) for i in range(3)]
                R = lambda n: Res(n)
                r_mt = [R(f"mt{i}") for i in range(2)]
                r_x3 = [R(f"x3{i}") for i in range(NX3)]
                r_o3 = [R(f"o3{i}") for i in range(NX3)]
                r_op = [R(f"op{i}") for i in range(3)]
                d_mt = [sch.dma_sem(f"d_mt{i}") for i in range(2)]
                d_x3 = [sch.dma_sem(f"d_x3{i}") for i in range(NX3)]
                d_o3 = [sch.dma_sem(f"d_o3{i}") for i in range(NX3)]
                MT_v = MTs.rearrange("c p t -> p c t")
                def ld_mt(n):
                    ms_ = n % 2
                    op(SP, lambda e: e.dma_start(out=mt[ms_][:], in_=MT_v[:, :, n * TB:(n + 1) * TB]),
                       reads=r_MTs, writes=[r_mt[ms_]], dsem=d_mt[ms_])

                def ld_x(k):
                    xs_ = k % NX3
                    row0 = k * 128
                    op(SP, lambda e: e.dma_start(out=x3[xs_][:], in_=x[row0:row0 + 128, :]),
                       writes=[r_x3[xs_]], dsem=d_x3[xs_])

                ld_mt(0)
                for k in range(NX3):
                    ld_x(k)
                for n in range(NTB):
                    ms_ = n % 2
                    if n + 1 < NTB:
                        ld_mt(n + 1)
                    for j in range(4):
                        k3 = n * 4 + j
                        xs_ = k3 % NX3
                        pb = k3 % 3
                        row0 = k3 * 128
                        for half in range(2):
                            for cc in range(8):
                                op(PE, lambda e, pb=pb, half=half, cc=cc, ms_=ms_, j=j: e.matmul(
                                    op_ps[pb][:, 512 * half:512 * half + 512], lhsT=mt[ms_][:, cc, j * 128:(j + 1) * 128],
                                    rhs=wo16[:, cc, 512 * half:512 * half + 512], start=(cc == 0), stop=(cc == 7)),
                                   reads=[r_mt[ms_], r_wo], writes=[r_op[pb]], partial=(half > 0 or cc > 0))
                        op(DVE, lambda e, pb=pb, xs_=xs_: e.tensor_tensor(out=o3[xs_][:], in0=op_ps[pb][:], in1=x3[xs_][:],
                                                                          op=ALU.add),
                           reads=[r_op[pb], r_x3[xs_]], writes=[r_o3[xs_]])
                        op(SP, lambda e, xs_=xs_, row0=row0: e.dma_start(out=out[row0:row0 + 128, :], in_=o3[xs_][:]),
                           reads=[r_o3[xs_]], dsem=d_o3[xs_])
                        if k3 + NX3 < 4 * NTB:
                            ld_x(k3 + NX3)
                sch.barrier()
                sch.flush()
    return nc


_NC_CACHE = {}


def _consts():
    half = 32
    inv = 10000.0 ** (-np.arange(half, dtype=np.float64) / half)
    ang = np.arange(S, dtype=np.float64)[:, None] * inv[None, :]
    cos = np.cos(ang).astype(np.float32)
    sin = np.sin(ang).astype(np.float32)
    idx = np.arange(128) % 32
    cosT = np.ascontiguousarray(cos[:, idx].T)
    sinT = np.ascontiguousarray(sin[:, idx].T)
    bf = ml_dtypes.bfloat16
    kk = np.arange(128)[:, None]
    qq = np.arange(128)[None, :]
    prev_b = (kk >= qq).astype(np.float32)
    prev_a = (kk > qq).astype(np.float32)
    diag = (kk <= qq).astype(np.float32)
    zero = np.zeros((128, 128), np.float32)

    def mk(prev, firsts):
        pm = lambda i: zero if i in firsts else prev
        parts = [diag, pm(1), diag, pm(2), diag, pm(3), diag, pm(0)]
        return np.concatenate(parts, axis=1)

    masks = np.stack([mk(prev_b, ()), mk(prev_b, (0,)), mk(prev_b, (0, 2)), mk(prev_a, ()), mk(prev_a, (0,))]).astype(bf)
    ident = np.eye(128, dtype=np.float32).astype(bf)
    head = np.arange(128) // 64
    bones = ((head[:, None] == head[None, :]).astype(np.float32) / 64.0).astype(bf)
    rmat = np.zeros((128, 128), np.float32)
    for do in range(128):
        h, d = do // 64, do % 64
        if d < 32:
            rmat[h * 64 + d + 32, do] = -1.0
        else:
            rmat[h * 64 + d - 32, do] = 1.0
    rmat = rmat.astype(bf)
    lmask = np.zeros((128, 8), np.float32)
    for h in range(8):
        if h % 2 == 0:
            lmask[64:128, h] = 1.0
        else:
            lmask[0:64, h] = 1.0
    return dict(cosT=cosT, sinT=sinT, masks=masks, ident=ident, bones=bones, rmat=rmat, lmask=lmask)


def kernel(x, norm_gain, w_in, q_norm_a, k_norm_a, sinks_a, q_norm_b, k_norm_b, w_out):
    x = np.asarray(x, dtype=np.float32)
    if "nc" not in _NC_CACHE:
        _NC_CACHE["nc"] = build_nc()
    nc = _NC_CACHE["nc"]
    cst = _consts()
    f32 = lambda a: np.ascontiguousarray(np.asarray(a, dtype=np.float32))
    gain_b = np.ascontiguousarray(np.broadcast_to(f32(norm_gain).reshape(1, D), (128, D)))
    gvec = np.stack([np.tile(f32(q_norm_a).reshape(64), 2), np.tile(f32(k_norm_a).reshape(64), 2),
                     np.tile(f32(q_norm_b).reshape(64), 2), np.tile(f32(k_norm_b).reshape(64), 2)], axis=1)
    sinks_rep = np.ascontiguousarray(np.broadcast_to(f32(sinks_a).reshape(1, 8), (128, 8)))
    common = dict(w_in=f32(w_in).reshape(D, E_IN), w_out=f32(w_out).reshape(D, D), gain_b=gain_b,
                  gvec=np.ascontiguousarray(gvec), sinks_rep=sinks_rep, **cst)
    in_maps = [dict(x=np.ascontiguousarray(x[i]), **common) for i in range(NCORES)]
    res = run_bass_kernel_spmd(nc, in_maps, core_ids=list(range(NCORES)))
    return np.stack([np.asarray(r["out"], dtype=np.float32) for r in res.results], axis=0)
```

```python
import math
import numpy as np
import ml_dtypes
import concourse.bass as bass
import concourse.mybir as mybir
from concourse.bass_utils import run_bass_kernel_spmd

F32 = mybir.dt.float32
BF16 = mybir.dt.bfloat16
AF = mybir.ActivationFunctionType
ALU = mybir.AluOpType

S = 4096
D = 1024
E_IN = 3328
NCORES = 8
EPS = 1e-6
TB = 512
NTB = S // TB

C_QA, C_KA, C_VA, C_GA, C_QB, C_KB, C_VB, C_GB = 0, 512, 640, 768, 1280, 1792, 2304, 2816


class Res:
    __slots__ = ("name", "w", "r")

    def __init__(self, name):
        self.name = name
        self.w = {}
        self.r = {}


class Eng:
    def __init__(self, name, sem, is_pe=False, is_queue=False):
        self.name = name
        self.sem = sem
        self.count = 0
        self.ops = []
        self.waited = {}
        self.is_pe = is_pe
        self.is_queue = is_queue


class DmaSem:
    def __init__(self, sem):
        self.sem = sem
        self.count = 0


class Sched:
    def __init__(self, nc, ctx):
        self.nc = nc
        self.ctx = ctx
        mk = lambda n: ctx.enter_context(nc.semaphore(n))
        self.pe = Eng("pe", mk("s_pe"), is_pe=True)
        self.act = Eng("act", mk("s_act"))
        self.dve = Eng("dve", mk("s_dve"))
        self.pool = Eng("pool", mk("s_pool"))
        self.sp = Eng("sp", mk("s_sp"), is_queue=True)
        self.engs = [self.pe, self.act, self.dve, self.pool, self.sp]
        self.dsems = []
        self.dsem_of = {}

    def dma_sem(self, name):
        d = DmaSem(self.ctx.enter_context(self.nc.semaphore(name)))
        self.dsems.append(d)
        self.dsem_of[d.sem] = d
        return d

    def op(self, eng, fn, reads=(), writes=(), dsem=None, partial=False):
        waits = {}

        def need(d):
            for s, v in d.items():
                if waits.get(s, 0) < v:
                    waits[s] = v

        for R in reads:
            need(R.w)
        for W in writes:
            need(W.w)
            need(W.r)
        final = []
        for s, v in waits.items():
            if eng.is_pe and s is eng.sem:
                continue
            if s in self.dsem_of:
                v = self.dsem_of[s].count
            if eng.waited.get(s, 0) >= v:
                continue
            eng.waited[s] = v
            final.append((s, v))
        if dsem is None:
            eng.count += 1
            ev = (eng.sem, eng.count)
            inc = (eng.sem, 1)
        else:
            dsem.count += 16
            ev = (dsem.sem, dsem.count)
            inc = (dsem.sem, 16)
        eng.ops.append((final, fn, inc))
        for R in reads:
            if R.r.get(ev[0], 0) < ev[1]:
                R.r[ev[0]] = ev[1]
        for W in writes:
            if partial:
                W.w[ev[0]] = ev[1]
            else:
                W.w = {ev[0]: ev[1]}
                W.r = {}
        return ev

    def barrier(self):
        allev = {}
        for e in self.engs:
            if not e.is_queue and e.count:
                allev[e.sem] = e.count
        for d in self.dsems:
            if d.count:
                allev[d.sem] = d.count
        for e in self.engs:
            final = []
            for s, v in allev.items():
                if e.waited.get(s, 0) >= v:
                    continue
                e.waited[s] = v
                final.append((s, v))
            if final:
                e.ops.append((final, None, None))

    def flush(self):
        nc = self.nc
        with nc.Block() as block:
            def replay(eng):
                def run(e):
                    for waits, fn, inc in eng.ops:
                        for s, v in waits:
                            e.wait_ge(s, v)
                        if fn is not None:
                            fn(e).then_inc(inc[0], inc[1])
                return run

            block.tensor(replay(self.pe))
            block.scalar(replay(self.act))
            block.vector(replay(self.dve))
            block.gpsimd(replay(self.pool))
            block.sync(replay(self.sp))
        for e in self.engs:
            e.ops = []


def build_nc(debug=False, stop_after=3):
    from contextlib import ExitStack

    nc = bass.Bass("TRN2", target_bir_lowering=False)
    dt = lambda name, shape, dtype, kind: nc.dram_tensor(name, shape, dtype, kind=kind).ap()
    x = dt("x", [S, D], F32, "ExternalInput")
    w_in = dt("w_in", [D, E_IN], F32, "ExternalInput")
    w_out = dt("w_out", [D, D], F32, "ExternalInput")
    gain_b = dt("gain_b", [128, D], F32, "ExternalInput")
    gvec = dt("gvec", [128, 4], F32, "ExternalInput")
    sinks_rep = dt("sinks_rep", [128, 8], F32, "ExternalInput")
    lmask = dt("lmask", [128, 8], F32, "ExternalInput")
    cosT = dt("cosT", [128, S], F32, "ExternalInput")
    sinT = dt("sinT", [128, S], F32, "ExternalInput")
    masks = dt("masks", [5, 128, 1024], BF16, "ExternalInput")
    ident = dt("ident", [128, 128], BF16, "ExternalInput")
    bones = dt("bones", [128, 128], BF16, "ExternalInput")
    rmat = dt("rmat", [128, 128], BF16, "ExternalInput")
    out = dt("out", [S, D], F32, "ExternalOutput")
    sk = "ExternalOutput" if debug else "Internal"
    QTs = dt("QTs", [8, 128, S], BF16, sk)
    KTs = dt("KTs", [5, 128, S], BF16, sk)
    GTs = dt("GTs", [8, 128, S], BF16, sk)
    Vs = dt("Vs", [S, 12, 128], BF16, sk)
    MTs = dt("MTs", [8, 128, S], BF16, sk)

    with ExitStack() as ctx:
        sch = Sched(nc, ctx)
        op = sch.op
        PE, ACT, DVE, POOL, SP = sch.pe, sch.act, sch.dve, sch.pool, sch.sp
        sb = lambda name, shape, dtype, c=ctx: c.enter_context(nc.sbuf_tensor(name, shape, dtype))
        ps = lambda name, shape, dtype, c=ctx: c.enter_context(nc.psum_tensor(name, shape, dtype))

        r_QTs = [Res(f"QTs{i}") for i in range(8)]
        r_KTs = [Res(f"KTs{i}") for i in range(5)]
        r_GTs = [Res(f"GTs{i}") for i in range(8)]
        r_Vs = Res("Vs")
        r_MTs = [Res(f"MTs{i}") for i in range(8)]

        ident_sb = sb("ident_sb", [128, 128], BF16)
        r_ident = Res("ident")
        wo16 = sb("wo16", [128, 8, D], BF16)
        r_wo = Res("wo")
        d_wo = sch.dma_sem("d_wo")
        d_const = sch.dma_sem("d_const")
        op(SP, lambda e: e.dma_start(out=ident_sb[:], in_=ident), writes=[r_ident], dsem=d_const)

        with ExitStack() as c1:
            sb1 = lambda n, s, d: sb(n, s, d, c1)
            ps1 = lambda n, s, d: ps(n, s, d, c1)
            w16 = sb1("w16", [128, 8, E_IN], BF16)
            gainb = sb1("gainb", [128, D], F32)
            gv = sb1("gv", [128, 4], F32)
            bones_sb = sb1("bones_sb", [128, 128], BF16)
            rmat_sb = sb1("rmat_sb", [128, 128], BF16)
            NXT = 4
            xt = [sb1(f"xt{i}", [128, D], F32) for i in range(NXT)]
            xs16 = [sb1(f"xs16_{i}", [128, D], BF16) for i in range(4)]
            junk16 = sb1("junk16", [128, D], BF16)
            ss = sb1("ss", [128, 64], F32)
            lnss = sb1("lnss", [128, 64], F32)
            rstd = sb1("rstd", [128, 64], F32)
            xsT = [sb1(f"xsT{i}", [128, 8, TB], BF16) for i in range(2)]
            cosb = [sb1(f"cosb{i}", [128, TB], F32) for i in range(2)]
            sinb = [sb1(f"sinb{i}", [128, TB], F32) for i in range(2)]
            NCH = 4
            sq16 = [sb1(f"sq16_{i}", [128, TB], BF16) for i in range(NCH)]
            qc16 = [sb1(f"qc16_{i}", [128, TB], BF16) for i in range(NCH)]
            lnv = [sb1(f"lnv{i}", [128, TB], F32) for i in range(NCH)]
            rinv = [sb1(f"rinv{i}", [128, TB], F32) for i in range(NCH)]
            ta = [sb1(f"ta{i}", [128, TB], F32) for i in range(NCH)]
            tb_ = [sb1(f"tb{i}", [128, TB], F32) for i in range(NCH)]
            tc = [sb1(f"tc{i}", [128, TB], F32) for i in range(NCH)]
            NST = 4
            o16 = [sb1(f"o16_{i}", [128, TB], BF16) for i in range(NST)]
            g16 = [sb1(f"g16_{i}", [128, TB], BF16) for i in range(NST)]
            vst = [sb1(f"vst{i}", [128, 12, 128], BF16) for i in range(2)]

            tr_ps = ps1("tr_ps", [128, 8, 128], BF16)
            NPJ = 3
            pj_ps = [ps1(f"pj_ps{i}", [128, TB], F32) for i in range(NPJ)]
            ms_ps = [ps1(f"ms_ps{i}", [128, TB], F32) for i in range(2)]
            rot_ps = [ps1(f"rot_ps{i}", [128, TB], F32) for i in range(2)]

            R = lambda n: Res(n)
            r_w16 = [R(f"w16_{i}") for i in range(7)]
            r_gainb, r_gv, r_bones, r_rmat = R("gainb"), R("gv"), R("bones"), R("rmat")
            r_xt = [R(f"xt{i}") for i in range(NXT)]
            r_xs16 = [R(f"xs16{i}") for i in range(4)]
            r_junk = R("junk")
            r_ss, r_lnss, r_rstd = R("ss"), R("lnss"), R("rstd")
            r_xsT = [R(f"xsT{i}") for i in range(2)]
            r_cos = [R(f"cos{i}") for i in range(2)]
            r_sin = [R(f"sin{i}") for i in range(2)]
            r_sq = [R(f"sq{i}") for i in range(NCH)]
            r_qc = [R(f"qc{i}") for i in range(NCH)]
            r_lnv = [R(f"lnv{i}") for i in range(NCH)]
            r_rinv = [R(f"rinv{i}") for i in range(NCH)]
            r_ta = [R(f"ta{i}") for i in range(NCH)]
            r_tb = [R(f"tb{i}") for i in range(NCH)]
            r_tc = [R(f"tc{i}") for i in range(NCH)]
            r_o16 = [R(f"o16{i}") for i in range(NST)]
            r_g16 = [R(f"g16{i}") for i in range(NST)]
            r_vst = [R(f"vst{i}") for i in range(2)]
            r_tr = R("tr_ps")
            r_pj = [R(f"pj{i}") for i in range(NPJ)]
            r_ms = [R(f"ms{i}") for i in range(2)]
            r_rot = [R(f"rot{i}") for i in range(2)]

            d_w = [sch.dma_sem(f"d_w{i}") for i in range(7)]
            d_xt = [sch.dma_sem(f"d_xt{i}") for i in range(NXT)]
            d_cs = [sch.dma_sem(f"d_cs{i}") for i in range(2)]
            d_o16 = [sch.dma_sem(f"d_o16{i}") for i in range(NST)]
            d_g16 = [sch.dma_sem(f"d_g16{i}") for i in range(NST)]
            d_vst = [sch.dma_sem(f"d_vst{i}") for i in range(2)]

            op(SP, lambda e: e.dma_start(out=gainb[:], in_=gain_b), writes=[r_gainb], dsem=d_const)
            op(SP, lambda e: e.dma_start(out=gv[:], in_=gvec), writes=[r_gv], dsem=d_const)
            op(SP, lambda e: e.dma_start(out=bones_sb[:], in_=bones), writes=[r_bones], dsem=d_const)
            op(SP, lambda e: e.dma_start(out=rmat_sb[:], in_=rmat), writes=[r_rmat], dsem=d_const)
            for i in range(2):
                op(POOL, lambda e, i=i: e.memset(vst[i][:], 1.0), writes=[r_vst[i]])
            w_view = w_in.rearrange("(c p) e -> p c e", p=128)
            wblocks = [(0, 512), (512, 1024), (1280, 1792), (1792, 2304), (2304, 2816), (1024, 1280), (2816, 3328)]
            wblk_of_col = {}
            for bi, (c0, c1_) in enumerate(wblocks):
                for cc in range(c0, c1_, 128):
                    wblk_of_col[cc] = bi
                op(POOL, lambda e, c0=c0, c1_=c1_: e.dma_start(out=w16[:, :, c0:c1_], in_=w_view[:, :, c0:c1_]),
                   writes=[r_w16[bi]], dsem=d_w[bi])
            op(POOL, lambda e: e.dma_start(out=wo16[:], in_=w_out.rearrange("(c p) e -> p c e", p=128)),
               writes=[r_wo], dsem=d_wo)


            def xload(n, j):
                bs = n % 2
                k = n * 4 + j
                sl = k % NXT
                row0 = n * TB + j * 128
                op(SP, lambda e: e.dma_start(out=xt[sl][:], in_=x[row0:row0 + 128, :]),
                   writes=[r_xt[sl]], dsem=d_xt[sl])
                if j == 0:
                    op(SP, lambda e: e.dma_start(out=cosb[bs][:], in_=cosT[:, n * TB:(n + 1) * TB]),
                       writes=[r_cos[bs]], dsem=d_cs[bs])
                    op(SP, lambda e: e.dma_start(out=sinb[bs][:], in_=sinT[:, n * TB:(n + 1) * TB]),
                       writes=[r_sin[bs]], dsem=d_cs[bs])

            def xprep1(n, j):
                bs = n % 2
                k = n * 4 + j
                sl = k % NXT
                s2 = k % 4
                col = k % 64
                op(ACT, lambda e: e.activation(out=junk16[:], in_=xt[sl][:], func=AF.Square,
                                               accum_out=ss[:, col:col + 1]),
                   reads=[r_xt[sl]], writes=[r_junk, r_ss], partial=True)
                op(ACT, lambda e: e.activation(out=lnss[:, col:col + 1], in_=ss[:, col:col + 1], func=AF.Ln,
                                               scale=1.0 / D, bias=EPS),
                   reads=[r_ss], writes=[r_lnss], partial=True)
                op(ACT, lambda e: e.activation(out=rstd[:, col:col + 1], in_=lnss[:, col:col + 1], func=AF.Exp,
                                               scale=-0.5),
                   reads=[r_lnss], writes=[r_rstd], partial=True)
                op(DVE, lambda e: e.scalar_tensor_tensor(
                    out=xs16[s2][:], in0=xt[sl][:], scalar=rstd[:, col:col + 1], in1=gainb[:],
                    op0=ALU.mult, op1=ALU.mult),
                   reads=[r_xt[sl], r_rstd, r_gainb], writes=[r_xs16[s2]])

            def xprep2(n, j):
                bs = n % 2
                s2 = (n * 4 + j) % 4
                for c in range(8):
                    op(PE, lambda e, c=c: e.transpose(out=tr_ps[:, c, :], in_=xs16[s2][:, c * 128:(c + 1) * 128],
                                                      identity=ident_sb[:]),
                       reads=[r_xs16[s2], r_ident], writes=[r_tr], partial=(c > 0))
                op(ACT, lambda e: e.activation(out=xsT[bs][:, :, j * 128:(j + 1) * 128], in_=tr_ps[:], func=AF.Copy),
                   reads=[r_tr], writes=[r_xsT[bs]], partial=(j > 0))

            pj_ctr = [0]
            ch_ctr = [0]
            st_ctr = [0]
            gst_ctr = [0]
            vst_ctr = [0]

            def proj_fm(n, col0):
                bs = n % 2
                k = pj_ctr[0]
                pj_ctr[0] += 1
                pslot = k % NPJ
                for dc in range(8):
                    op(PE, lambda e, pslot=pslot, dc=dc, bs=bs: e.matmul(
                        pj_ps[pslot][:], lhsT=w16[:, dc, col0:col0 + 128], rhs=xsT[bs][:, dc, :],
                        start=(dc == 0), stop=(dc == 7)),
                       reads=[r_w16[wblk_of_col[col0]], r_xsT[bs]], writes=[r_pj[pslot]], partial=(dc > 0))
                return pslot

            def qk_chain_a(n, pslot, gcol):
                k = ch_ctr[0]
                ch_ctr[0] += 1
                cs = k % NCH
                m2 = k % 2
                op(ACT, lambda e: e.activation(out=sq16[cs][:], in_=pj_ps[pslot][:], func=AF.Square),
                   reads=[r_pj[pslot]], writes=[r_sq[cs]])
                op(ACT, lambda e: e.activation(out=qc16[cs][:], in_=pj_ps[pslot][:], func=AF.Identity,
                                               scale=gv[:, gcol:gcol + 1]),
                   reads=[r_pj[pslot], r_gv], writes=[r_qc[cs]])
                return (k, cs, m2)

            def qk_chain_pe(state):
                k, cs, m2 = state
                op(PE, lambda e: e.matmul(ms_ps[m2][:], lhsT=bones_sb[:], rhs=sq16[cs][:], start=True, stop=True),
                   reads=[r_bones, r_sq[cs]], writes=[r_ms[m2]])
                op(PE, lambda e: e.matmul(rot_ps[m2][:], lhsT=rmat_sb[:], rhs=qc16[cs][:], start=True, stop=True),
                   reads=[r_rmat, r_qc[cs]], writes=[r_rot[m2]])

            def qk_chain_b1(n, pslot, gcol, state):
                k, cs, m2 = state
                bs = n % 2
                op(DVE, lambda e: e.scalar_tensor_tensor(out=ta[cs][:], in0=pj_ps[pslot][:], scalar=gv[:, gcol:gcol + 1],
                                                         in1=cosb[bs][:], op0=ALU.mult, op1=ALU.mult),
                   reads=[r_pj[pslot], r_gv, r_cos[bs], r_qc[cs]], writes=[r_ta[cs]])
                op(DVE, lambda e: e.tensor_tensor(out=tb_[cs][:], in0=rot_ps[m2][:], in1=sinb[bs][:], op=ALU.mult),
                   reads=[r_rot[m2], r_sin[bs]], writes=[r_tb[cs]])
                op(POOL, lambda e: e.tensor_tensor(out=tc[cs][:], in0=ta[cs][:], in1=tb_[cs][:], op=ALU.add),
                   reads=[r_ta[cs], r_tb[cs]], writes=[r_tc[cs]])

            def qk_chain_b2(n, is_q, state, dst, r_dst):
                k, cs, m2 = state
                op(ACT, lambda e: e.activation(out=lnv[cs][:], in_=ms_ps[m2][:], func=AF.Ln, bias=EPS),
                   reads=[r_ms[m2]], writes=[r_lnv[cs]])
                op(ACT, lambda e: e.activation(out=rinv[cs][:], in_=lnv[cs][:], func=AF.Exp, scale=-0.5,
                                               bias=(-math.log(8.0) if is_q else 0.0)),
                   reads=[r_lnv[cs]], writes=[r_rinv[cs]])
                s = st_ctr[0] % NST
                st_ctr[0] += 1
                op(DVE, lambda e: e.tensor_tensor(out=o16[s][:], in0=tc[cs][:], in1=rinv[cs][:], op=ALU.mult),
                   reads=[r_tc[cs], r_rinv[cs]], writes=[r_o16[s]])
                op(SP, lambda e: e.dma_start(out=dst[:, n * TB:(n + 1) * TB], in_=o16[s][:]),
                   reads=[r_o16[s]], writes=[r_dst], dsem=d_o16[s], partial=True)

            def gate_chunk(n, pslot, dst, r_dst):
                s = gst_ctr[0] % NST
                gst_ctr[0] += 1
                op(ACT, lambda e: e.activation(out=g16[s][:], in_=pj_ps[pslot][:], func=AF.Silu),
                   reads=[r_pj[pslot]], writes=[r_g16[s]])
                op(SP, lambda e: e.dma_start(out=dst[:, n * TB:(n + 1) * TB], in_=g16[s][:]),
                   reads=[r_g16[s]], writes=[r_dst], dsem=d_g16[s], partial=True)

            def v_tile(n, j):
                bs = n % 2
                k = pj_ctr[0]
                pj_ctr[0] += 2
                pa, pb = k % NPJ, (k + 1) % NPJ
                for dc in range(8):
                    op(PE, lambda e, dc=dc: e.matmul(pj_ps[pb][:], lhsT=xsT[bs][:, dc, j * 128:(j + 1) * 128],
                                                     rhs=w16[:, dc, C_VB:C_VB + 512], start=(dc == 0), stop=(dc == 7)),
                       reads=[r_w16[wblk_of_col[C_VB]], r_xsT[bs]], writes=[r_pj[pb]], partial=(dc > 0))
                for dc in range(8):
                    op(PE, lambda e, dc=dc: e.matmul(pj_ps[pa][:, 0:128], lhsT=xsT[bs][:, dc, j * 128:(j + 1) * 128],
                                                     rhs=w16[:, dc, C_VA:C_VA + 128], start=(dc == 0), stop=(dc == 7)),
                       reads=[r_w16[wblk_of_col[C_VA]], r_xsT[bs]], writes=[r_pj[pa]], partial=(dc > 0))
                vs = vst_ctr[0] % 2
                vst_ctr[0] += 1
                vv = vst[vs][:].rearrange("p (m two) (h e) -> p m (two h) e", two=2, h=2)
                op(DVE, lambda e: e.tensor_copy(out=vv[:, 2:6, 0:4:3, :],
                                                in_=pj_ps[pb][:].rearrange("p (m two e) -> p m two e", two=2, e=64)),
                   reads=[r_pj[pb]], writes=[r_vst[vs]])
                va_in = pj_ps[pa][:, 0:128].rearrange("p (k e) -> p k e", e=64)
                op(DVE, lambda e: e.tensor_copy(out=vv[:, 0:2, 0, :], in_=va_in),
                   reads=[r_pj[pa]], writes=[r_vst[vs]], partial=True)
                op(DVE, lambda e: e.tensor_copy(out=vv[:, 0:2, 3, :], in_=va_in),
                   reads=[r_pj[pa]], writes=[r_vst[vs]], partial=True)
                row0 = n * TB + j * 128
                op(SP, lambda e: e.dma_start(out=Vs[row0:row0 + 128, :, :], in_=vst[vs][:]),
                   reads=[r_vst[vs]], writes=[r_Vs], dsem=d_vst[vs], partial=True)

            chunks = []
            for i in range(4):
                chunks.append(("q", C_QA + 128 * i, 0, QTs[i], r_QTs[i]))
            chunks.append(("k", C_KA, 1, KTs[0], r_KTs[0]))
            for i in range(4):
                chunks.append(("q", C_QB + 128 * i, 2, QTs[4 + i], r_QTs[4 + i]))
            for i in range(4):
                chunks.append(("k", C_KB + 128 * i, 3, KTs[1 + i], r_KTs[1 + i]))
            gchunks = []
            for i in range(4):
                gchunks.append((C_GA + 128 * i, GTs[i], r_GTs[i]))
            for i in range(4):
                gchunks.append((C_GB + 128 * i, GTs[4 + i], r_GTs[4 + i]))

            for j in range(4):
                xload(0, j)
            for j in range(4):
                xprep1(0, j)
            for j in range(4):
                xprep2(0, j)
            NQK = len(chunks)
            for n in range(NTB):
                st8 = {}

                def s1(i):
                    kind, col0, gcol, dst, r_dst = chunks[i]
                    pslot = proj_fm(n, col0)
                    st8[i] = (pslot, qk_chain_a(n, pslot, gcol))

                def s2(i):
                    pslot, state = st8[i]
                    qk_chain_pe(state)
                    qk_chain_b1(n, pslot, chunks[i][2], state)

                def s3(i):
                    pslot, state = st8[i]
                    qk_chain_b2(n, chunks[i][0] == "q", state, chunks[i][3], chunks[i][4])

                if n + 1 < NTB:
                    for j in range(4):
                        xload(n + 1, j)
                for i in range(NQK):
                    s1(i)
                    if i >= 1:
                        s2(i - 1)
                    if i >= 2:
                        s3(i - 2)
                    if n + 1 < NTB and i in (1, 4, 7, 10):
                        xprep1(n + 1, (i - 1) // 3)
                v_tile(n, 0)
                s2(NQK - 1)
                v_tile(n, 1)
                s3(NQK - 2)
                v_tile(n, 2)
                s3(NQK - 1)
                v_tile(n, 3)
                for gi, (col0, dst, r_dst) in enumerate(gchunks):
                    pslot = proj_fm(n, col0)
                    gate_chunk(n, pslot, dst, r_dst)
                    if n + 1 < NTB and gi < 4:
                        xprep2(n + 1, gi)
            sch.barrier()
            sch.flush()

        if stop_after >= 2:
            with ExitStack() as c2:
                sb2 = lambda n, s, d: sb(n, s, d, c2)
                ps2 = lambda n, s, d: ps(n, s, d, c2)
                mask_sb = sb2("mask_sb", [128, 5, 1024], BF16)
                sinkb = sb2("sinkb", [128, 8], F32)
                lmask_sb = sb2("lmask_sb", [128, 8], F32)
                QTc = [sb2(f"QTc{i}", [128, S], BF16) for i in range(2)]
                KTc = [sb2(f"KTc{i}", [128, S], BF16) for i in range(2)]
                Gc = [sb2(f"Gc{i}", [128, S], BF16) for i in range(2)]
                NV = 4
                Vt = [sb2(f"Vt{i}", [128, 32, 256], BF16) for i in range(NV)]
                acc = [sb2(f"acc{i}", [128, S], F32) for i in range(2)]
                Rt = sb2("Rt", [128, 4, 512], F32)
                Mx = sb2("Mx", [128, 4, 512], BF16)
                QTf = {4: sb2("QTf4", [128, S], BF16), 16: sb2("QTf16", [128, S], BF16)}
                r_QTf = {4: Res("QTf4"), 16: Res("QTf16")}
                NP = 4
                Pb = [sb2(f"Pb{i}", [128, 1024], BF16) for i in range(NP)]
                NSTB = 2
                st_ps = [ps2(f"st_ps{i}", [128, 1024], F32) for i in range(NSTB)]
                ot_ps = [ps2(f"ot_ps{i}", [128, 512], F32) for i in range(2)]
                dm_ps = ps2("dm_ps", [128, 512], F32)
                r_dm = Res("dm")
                NFILL = 1

                R = lambda n: Res(n)
                r_mask, r_sinkb, r_lmask = R("mask"), R("sinkb"), R("lmask")
                r_QTc = [R(f"QTc{i}") for i in range(2)]
                r_KTc = [R(f"KTc{i}") for i in range(2)]
                r_Gc = [R(f"Gc{i}") for i in range(2)]
                r_Vt = [R(f"Vt{i}") for i in range(NV)]
                r_acc = [R(f"acc{i}") for i in range(2)]
                r_Rt = R("Rt")
                r_Mx = [R(f"Mx{i}") for i in range(4)]
                r_Pb = [R(f"Pb{i}") for i in range(NP)]
                r_st = [R(f"st{i}") for i in range(NSTB)]
                r_ot = [R(f"ot{i}") for i in range(2)]
                d_c2 = sch.dma_sem("d_c2")
                d_q = [sch.dma_sem(f"d_q{i}") for i in range(2)]
                d_k = [sch.dma_sem(f"d_k{i}") for i in range(2)]
                d_g = [sch.dma_sem(f"d_g{i}") for i in range(2)]
                d_v = [sch.dma_sem(f"d_v{i}") for i in range(NV)]
                d_mx = [sch.dma_sem(f"d_mx{i}") for i in range(4)]

                op(SP, lambda e: e.dma_start(out=mask_sb[:], in_=masks.rearrange("m p f -> p m f")),
                   writes=[r_mask], dsem=d_c2)
                op(SP, lambda e: e.dma_start(out=sinkb[:], in_=sinks_rep), writes=[r_sinkb], dsem=d_c2)
                op(SP, lambda e: e.dma_start(out=lmask_sb[:], in_=lmask), writes=[r_lmask], dsem=d_c2)
                op(ACT, lambda e: e.activation(out=sinkb[:], in_=sinkb[:], func=AF.Exp), reads=[], writes=[r_sinkb])
                op(DVE, lambda e: e.tensor_tensor(out=sinkb[:], in0=sinkb[:], in1=lmask_sb[:], op=ALU.mult),
                   reads=[r_lmask], writes=[r_sinkb])

                Vs_flat = Vs.rearrange("t s e -> t (s e)")
                bt_ctr = [0]

                def load_pair(c):
                    sl = c % 2
                    op(SP, lambda e: e.dma_start(out=QTc[sl][:], in_=QTs[c]), reads=[r_QTs[c]], writes=[r_QTc[sl]],
                       dsem=d_q[sl])
                    if c < 4:
                        kv = c // 2
                        op(SP, lambda e: e.dma_start(out=KTc[sl][0:64, :], in_=KTs[0][64 * kv:64 * kv + 64, :]),
                           reads=[r_KTs[0]], writes=[r_KTc[sl]], dsem=d_k[sl])
                        op(SP, lambda e: e.dma_start(out=KTc[sl][64:128, :], in_=KTs[0][64 * kv:64 * kv + 64, :]),
                           reads=[r_KTs[0]], writes=[r_KTc[sl]], dsem=d_k[sl], partial=True)
                    else:
                        op(SP, lambda e: e.dma_start(out=KTc[sl][:], in_=KTs[1 + (c - 4)]), reads=[r_KTs[1 + (c - 4)]],
                           writes=[r_KTc[sl]], dsem=d_k[sl])

                def load_G(c):
                    sl = c % 2
                    op(SP, lambda e: e.dma_start(out=Gc[sl][:], in_=GTs[c]), reads=[r_GTs[c]], writes=[r_Gc[sl]],
                       dsem=d_g[sl])

                def load_v(c, dil):
                    vs_ = {1: (0 if c % 2 == 0 else 3), 4: 1, 16: 2}[dil]
                    slot0 = 2 * (c // 2) if c < 4 else 4 + 2 * (c - 4)
                    f0 = slot0 * 128
                    nb = 32 // dil
                    first = True
                    for r in range(dil):
                        src = Vs_flat.rearrange("(b i r) f -> r i b f", i=128, r=dil)[r][:, :, f0:f0 + 256]
                        op(SP, lambda e, src=src, r=r: e.dma_start(out=Vt[vs_][:, r * nb:(r + 1) * nb, :], in_=src),
                           reads=[r_Vs], writes=[r_Vt[vs_]], dsem=d_v[vs_], partial=not first)
                        first = False
                    return vs_

                NBLK = S // 512
                r_accb = [[Res(f"acc{h}_{j}") for j in range(NBLK)] for h in range(2)]

                def make_job(c, sl, hl, dil, vs_, is_a, first_pat, bt):
                    nb = 32 // dil
                    k = bt_ctr[0]
                    bt_ctr[0] += 1
                    units = [4 * bt + i for i in range(4)]
                    firsts = [(u % nb) == 0 for u in units]
                    if is_a:
                        mi = 4 if firsts[0] else 3
                    elif firsts[0] and firsts[2]:
                        mi = 2
                    elif firsts[0]:
                        mi = 1
                    else:
                        mi = 0
                    return dict(c=c, sl=sl, hl=hl, dil=dil, vs_=vs_, is_a=is_a, first_pat=first_pat, bt=bt, nb=nb,
                                p0=64 * hl, stb=k % NSTB, pb=k % NP, ob=k % 2, units=units, firsts=firsts, mi=mi)

                def stage_a(J):
                    sl, dil, nb, p0, stb, pb, mi = J["sl"], J["dil"], J["nb"], J["p0"], J["stb"], J["pb"], J["mi"]

                    def tok(u):
                        r, b = u // nb, u % nb
                        start = r + dil * 128 * b
                        return slice(start, start + dil * 127 + 1, dil)

                    units, firsts = J["units"], J["firsts"]

                    def tok2(u):
                        r, b = u // nb, u % nb
                        start = r + dil * 128 * b
                        return slice(start, start + dil * 255 + 1, dil)

                    def qk(kunit, qunit, pos, n, first):
                        if dil == 1:
                            start = 128 * qunit
                            rhs = QTc[sl][p0:p0 + 64, start:start + n]
                            rq = r_QTc[sl]
                        else:
                            r, b = qunit // nb, qunit % nb
                            start = r * (S // dil) + 128 * b
                            rhs = QTf[dil][p0:p0 + 64, start:start + n]
                            rq = r_QTf[dil]
                        op(PE, lambda e: e.matmul(
                            st_ps[stb][:, 128 * pos:128 * pos + n], lhsT=KTc[sl][p0:p0 + 64, tok(kunit)],
                            rhs=rhs, start=True, stop=True),
                           reads=[r_KTc[sl], rq], writes=[r_st[stb]], partial=not first)

                    u0 = units[0]
                    qk(u0 if u0 == 0 else u0 - 1, u0, 7, 128, True)
                    for i in range(3):
                        if not firsts[i + 1]:
                            qk(units[i], units[i], 2 * i, 256, False)
                        else:
                            qk(units[i], units[i], 2 * i, 128, False)
                            qk(units[i], units[i + 1], 2 * i + 1, 128, False)
                    qk(units[3], units[3], 6, 128, False)
                    for _ in range(NFILL):
                        op(PE, lambda e: e.matmul(dm_ps[:], lhsT=mask_sb[:, 0, 0:128], rhs=mask_sb[:, 1, 0:512],
                                                  start=True, stop=True),
                           reads=[r_mask], writes=[r_dm])
                    op(ACT, lambda e: e.activation(out=Pb[pb][:], in_=st_ps[stb][:], func=AF.Exp),
                       reads=[r_st[stb]], writes=[r_Pb[pb]])
                    op(DVE, lambda e: e.tensor_tensor(out=Pb[pb][:], in0=Pb[pb][:], in1=mask_sb[:, mi, :], op=ALU.mult),
                       reads=[r_mask], writes=[r_Pb[pb]])

                def stage_b(J):
                    c, hl, dil, nb, pb, ob, vs_ = J["c"], J["hl"], J["dil"], J["nb"], J["pb"], J["ob"], J["vs_"]
                    units = J["units"]
                    a_t = acc[hl]
                    def pv(vunit, pos, col, n, start, stop, first):
                        op(PE, lambda e: e.matmul(
                            ot_ps[ob][:, col:col + n], lhsT=Vt[vs_][:, vunit, 128 * hl:128 * hl + 128],
                            rhs=Pb[pb][:, 128 * pos:128 * pos + n], start=start, stop=stop, skip_group_check=True),
                           reads=[r_Vt[vs_], r_Pb[pb]], writes=[r_ot[ob]], partial=not first)

                    u0 = units[0]
                    pv(u0 if u0 == 0 else u0 - 1, 7, 0, 128, True, False, True)
                    for i in range(3):
                        pv(units[i], 2 * i, 128 * i, 256, False, False, False)
                    pv(units[3], 6, 384, 128, False, True, False)
                    if dil == 16:
                        r0 = units[0] // nb
                        dst = a_t[:].rearrange("p (j r) -> p r j", r=16)[:, r0:r0 + 2, :]
                        src = ot_ps[ob][:].rearrange("p (r j) -> p r j", r=2)
                        blks = list(range(NBLK))
                    else:
                        r, b0 = units[0] // nb, units[0] % nb
                        start = r + dil * 128 * b0
                        dst = a_t[:, start:start + dil * 511 + 1:dil]
                        src = ot_ps[ob][:]
                        blks = list(range(start // 512, (start + dil * 511) // 512 + 1))
                    wr = [r_accb[hl][j] for j in blks]
                    if J["first_pat"]:
                        if J["is_a"]:
                            h = 2 * c + hl
                            op(DVE, lambda e: e.tensor_scalar(out=dst, in0=src, scalar1=sinkb[:, h:h + 1], scalar2=None,
                                                              op0=ALU.add),
                               reads=[r_ot[ob], r_sinkb], writes=wr, partial=True)
                        else:
                            op(DVE, lambda e: e.tensor_copy(out=dst, in_=src),
                               reads=[r_ot[ob]], writes=wr, partial=True)
                    else:
                        op(DVE, lambda e: e.tensor_tensor(out=dst, in0=src, in1=dst, op=ALU.add),
                           reads=[r_ot[ob]], writes=wr, partial=True)

                def fin_act(c, sl, j):
                    cs_ = slice(512 * j, 512 * j + 512)
                    jb = j % 4
                    rR = r_Rtb[jb]
                    op(ACT, lambda e: e.activation(out=Rt[0:64, jb, :], in_=acc[0][64:128, cs_], func=AF.Ln),
                       reads=[r_accb[0][j]], writes=[rR])
                    op(ACT, lambda e: e.activation(out=Rt[64:128, jb, :], in_=acc[1][0:64, cs_], func=AF.Ln),
                       reads=[r_accb[1][j]], writes=[rR], partial=True)
                    op(ACT, lambda e: e.activation(out=Rt[:, jb, :], in_=Rt[:, jb, :], func=AF.Exp, scale=-1.0),
                       reads=[], writes=[rR])

                def fin_pool(c, sl, j):
                    cs_ = slice(512 * j, 512 * j + 512)
                    jb = j % 4
                    op(POOL, lambda e: e.tensor_tensor(out=Rt[:, jb, :], in0=Rt[:, jb, :], in1=Gc[sl][:, cs_], op=ALU.mult),
                       reads=[r_Gc[sl]], writes=[r_Rtb[jb]])

                def fin_dve(c, sl, j):
                    cs_ = slice(512 * j, 512 * j + 512)
                    jb = j % 4
                    rR = r_Rtb[jb]
                    op(DVE, lambda e: e.tensor_tensor(out=Mx[0:64, jb, :], in0=acc[0][0:64, cs_], in1=Rt[0:64, jb, :],
                                                      op=ALU.mult),
                       reads=[r_accb[0][j], rR], writes=[r_Mx[jb]])
                    op(DVE, lambda e: e.tensor_tensor(out=Mx[64:128, jb, :], in0=acc[1][64:128, cs_],
                                                      in1=Rt[64:128, jb, :], op=ALU.mult),
                       reads=[r_accb[1][j], rR], writes=[r_Mx[jb]], partial=True)

                def fin_store(c, sl, j):
                    cs_ = slice(512 * j, 512 * j + 512)
                    jb = j % 4
                    op(ACT, lambda e: e.dma_start(out=MTs[c][:, cs_], in_=Mx[:, jb, :]),
                       reads=[r_Mx[jb]], writes=[r_MTs[c]], dsem=d_mx[jb], partial=(j > 0))

                FIN_STAGES = [fin_act, fin_pool, fin_dve, fin_store]

                act_bg = []

                def deinterleave(c, dil, spread=False):
                    sl = c % 2
                    L = S // dil
                    nparts = 4
                    rr = dil // nparts
                    if dil == 16 and spread:
                        src = QTf[4][:].rearrange("p (r i a) -> p a r i", r=4, a=4)
                        dst = QTf[16][:].rearrange("p (a r i) -> p a r i", a=4, r=4)
                        rsrc = r_QTf[4]

                        def piece(q):
                            op(ACT, lambda e: e.activation(out=dst[:, q, :, :], in_=src[:, q, :, :], func=AF.Copy),
                               reads=[rsrc], writes=[r_QTf[dil]], partial=(q > 0))
                    else:
                        src = QTc[sl][:].rearrange("p (i r) -> p r i", r=dil)
                        dst = QTf[dil][:].rearrange("p (r i) -> p r i", r=dil)

                        def piece(q):
                            op(ACT, lambda e: e.activation(out=dst[:, q * rr:(q + 1) * rr, :],
                                                           in_=src[:, q * rr:(q + 1) * rr, :], func=AF.Copy),
                               reads=[r_QTc[sl]], writes=[r_QTf[dil]], partial=(q > 0))

                    for q in range(nparts):
                        if spread:
                            act_bg.append(((c, dil), lambda q=q: piece(q)))
                        else:
                            piece(q)

                r_Rtb = [Res(f"Rt{i}") for i in range(4)]

                def dils_of(c):
                    return [1] if c < 4 else [1, 4, 16]

                load_pair(0)
                load_G(0)
                load_G(1)
                vslot = {}
                for d in dils_of(0):
                    vslot[(0, d)] = load_v(0, d)
                seq = []
                for c in range(8):
                    sl = c % 2
                    is_a = c < 4
                    dils = dils_of(c)
                    if c + 1 < 8:
                        def pre(c=c):
                            load_pair(c + 1)
                            vslot[(c + 1, 1)] = load_v(c + 1, 1)
                        seq.append(("pre", pre))
                    for hl in range(2):
                        for pi, dil in enumerate(dils):
                            for bt in range(8):
                                seq.append(("job", (c, sl, hl, dil, is_a, pi == 0, bt)))
                            if hl == 1 and dil > 1 and c + 1 < 8:
                                def post_v(c=c, dil=dil):
                                    vslot[(c + 1, dil)] = load_v(c + 1, dil)
                                    deinterleave(c + 1, dil, spread=True)
                                seq.append(("post_deferred", post_v))
                    if c == 3:
                        def post3():
                            for d in (4, 16):
                                vslot[(4, d)] = load_v(4, d)
                                deinterleave(4, d, spread=True)
                        seq.append(("post_deferred", post3))
                    seq.append(("fin", (c, sl)))
                LAG = 2
                pending = []

                fin_state = {"c": None, "sl": None, "t": 0}

                def fin_active():
                    return fin_state["c"] is not None

                def fin_step():
                    c_, sl_, t = fin_state["c"], fin_state["sl"], fin_state["t"]
                    for st_i in (3, 2, 1, 0):
                        j_ = t - st_i
                        if 0 <= j_ < NBLK:
                            FIN_STAGES[st_i](c_, sl_, j_)
                    fin_state["t"] = t + 1
                    if t + 1 >= NBLK + 3:
                        fin_state["c"] = None
                        if c_ + 2 < 8:
                            load_G(c_ + 2)

                deferred = []
                jcount = [0]

                def do_stage_b(J):
                    stage_b(J)
                    while deferred and deferred[0][0] <= J["idx"]:
                        deferred.pop(0)[1]()

                def drain():
                    while pending:
                        do_stage_b(pending.pop(0))
                    while deferred:
                        deferred.pop(0)[1]()

                for kind, item in seq:
                    if kind == "pre":
                        item()
                    elif kind == "job":
                        c, sl, hl, dil, is_a, fp, bt = item
                        while any(tag == (c, dil) for tag, _ in act_bg):
                            act_bg.pop(0)[1]()
                        J = make_job(c, sl, hl, dil, vslot[(c, dil)], is_a, fp, bt)
                        J["idx"] = jcount[0]
                        jcount[0] += 1
                        stage_a(J)
                        pending.append(J)
                        if act_bg and not fin_active():
                            act_bg.pop(0)[1]()
                        if fin_active():
                            fin_step()
                        if len(pending) > LAG:
                            do_stage_b(pending.pop(0))
                    elif kind == "post_deferred":
                        deferred.append((jcount[0] - 1, item))
                    elif kind == "post":
                        drain()
                        item()
                    else:
                        drain()
                        while fin_active():
                            fin_step()
                        c, sl = item
                        fin_state.update(c=c, sl=sl, t=0)
                drain()
                while act_bg:
                    act_bg.pop(0)[1]()
                while fin_active():
                    fin_step()
                sch.barrier()
                sch.flush()

        if stop_after >= 3:
            with ExitStack() as c3:
                sb3 = lambda n, s, d: sb(n, s, d, c3)
                ps3 = lambda n, s, d: ps(n, s, d, c3)
                mt = [sb3(f"mt{i}", [128, 8, TB], BF16) for i in range(2)]
                NX3 = 4
                x3 = [sb3(f"x3_{i}", [128, D], F32) for i in range(NX3)]
                o3 = [sb3(f"o3_{i}", [128, D], F32) for i in range(NX3)]
                op_ps = [ps3(f"op_ps{i}", [128, D], F32) for i in range(3)]
                R = lambda n: Res(n)
                r_mt = [R(f"mt{i}") for i in range(2)]
                r_x3 = [R(f"x3{i}") for i in range(NX3)]
                r_o3 = [R(f"o3{i}") for i in range(NX3)]
                r_op = [R(f"op{i}") for i in range(3)]
                d_mt = [sch.dma_sem(f"d_mt{i}") for i in range(2)]
                d_x3 = [sch.dma_sem(f"d_x3{i}") for i in range(NX3)]
                d_o3 = [sch.dma_sem(f"d_o3{i}") for i in range(NX3)]
                MT_v = MTs.rearrange("c p t -> p c t")
                def ld_mt(n):
                    ms_ = n % 2
                    op(SP, lambda e: e.dma_start(out=mt[ms_][:], in_=MT_v[:, :, n * TB:(n + 1) * TB]),
                       reads=r_MTs, writes=[r_mt[ms_]], dsem=d_mt[ms_])

                def ld_x(k):
                    xs_ = k % NX3
                    row0 = k * 128
                    op(SP, lambda e: e.dma_start(out=x3[xs_][:], in_=x[row0:row0 + 128, :]),
                       writes=[r_x3[xs_]], dsem=d_x3[xs_])

                ld_mt(0)
                for k in range(NX3):
                    ld_x(k)
                for n in range(NTB):
                    ms_ = n % 2
                    if n + 1 < NTB:
                        ld_mt(n + 1)
                    for j in range(4):
                        k3 = n * 4 + j
                        xs_ = k3 % NX3
                        pb = k3 % 3
                        row0 = k3 * 128
                        for half in range(2):
                            for cc in range(8):
                                op(PE, lambda e, pb=pb, half=half, cc=cc, ms_=ms_, j=j: e.matmul(
                                    op_ps[pb][:, 512 * half:512 * half + 512], lhsT=mt[ms_][:, cc, j * 128:(j + 1) * 128],
                                    rhs=wo16[:, cc, 512 * half:512 * half + 512], start=(cc == 0), stop=(cc == 7)),
                                   reads=[r_mt[ms_], r_wo], writes=[r_op[pb]], partial=(half > 0 or cc > 0))
                        op(DVE, lambda e, pb=pb, xs_=xs_: e.tensor_tensor(out=o3[xs_][:], in0=op_ps[pb][:], in1=x3[xs_][:],
                                                                          op=ALU.add),
                           reads=[r_op[pb], r_x3[xs_]], writes=[r_o3[xs_]])
                        op(SP, lambda e, xs_=xs_, row0=row0: e.dma_start(out=out[row0:row0 + 128, :], in_=o3[xs_][:]),
                           reads=[r_o3[xs_]], dsem=d_o3[xs_])
                        if k3 + NX3 < 4 * NTB:
                            ld_x(k3 + NX3)
                sch.barrier()
                sch.flush()
    return nc


_NC_CACHE = {}


def _consts():
    half = 32
    inv = 10000.0 ** (-np.arange(half, dtype=np.float64) / half)
    ang = np.arange(S, dtype=np.float64)[:, None] * inv[None, :]
    cos = np.cos(ang).astype(np.float32)
    sin = np.sin(ang).astype(np.float32)
    idx = np.arange(128) % 32
    cosT = np.ascontiguousarray(cos[:, idx].T)
    sinT = np.ascontiguousarray(sin[:, idx].T)
    bf = ml_dtypes.bfloat16
    kk = np.arange(128)[:, None]
    qq = np.arange(128)[None, :]
    prev_b = (kk >= qq).astype(np.float32)
    prev_a = (kk > qq).astype(np.float32)
    diag = (kk <= qq).astype(np.float32)
    zero = np.zeros((128, 128), np.float32)

    def mk(prev, firsts):
        pm = lambda i: zero if i in firsts else prev
        parts = [diag, pm(1), diag, pm(2), diag, pm(3), diag, pm(0)]
        return np.concatenate(parts, axis=1)

    masks = np.stack([mk(prev_b, ()), mk(prev_b, (0,)), mk(prev_b, (0, 2)), mk(prev_a, ()), mk(prev_a, (0,))]).astype(bf)
    ident = np.eye(128, dtype=np.float32).astype(bf)
    head = np.arange(128) // 64
    bones = ((head[:, None] == head[None, :]).astype(np.float32) / 64.0).astype(bf)
    rmat = np.zeros((128, 128), np.float32)
    for do in range(128):
        h, d = do // 64, do % 64
        if d < 32:
            rmat[h * 64 + d + 32, do] = -1.0
        else:
            rmat[h * 64 + d - 32, do] = 1.0
    rmat = rmat.astype(bf)
    lmask = np.zeros((128, 8), np.float32)
    for h in range(8):
        if h % 2 == 0:
            lmask[64:128, h] = 1.0
        else:
            lmask[0:64, h] = 1.0
    return dict(cosT=cosT, sinT=sinT, masks=masks, ident=ident, bones=bones, rmat=rmat, lmask=lmask)


def kernel(x, norm_gain, w_in, q_norm_a, k_norm_a, sinks_a, q_norm_b, k_norm_b, w_out):
    x = np.asarray(x, dtype=np.float32)
    if "nc" not in _NC_CACHE:
        _NC_CACHE["nc"] = build_nc()
    nc = _NC_CACHE["nc"]
    cst = _consts()
    f32 = lambda a: np.ascontiguousarray(np.asarray(a, dtype=np.float32))
    gain_b = np.ascontiguousarray(np.broadcast_to(f32(norm_gain).reshape(1, D), (128, D)))
    gvec = np.stack([np.tile(f32(q_norm_a).reshape(64), 2), np.tile(f32(k_norm_a).reshape(64), 2),
                     np.tile(f32(q_norm_b).reshape(64), 2), np.tile(f32(k_norm_b).reshape(64), 2)], axis=1)
    sinks_rep = np.ascontiguousarray(np.broadcast_to(f32(sinks_a).reshape(1, 8), (128, 8)))
    common = dict(w_in=f32(w_in).reshape(D, E_IN), w_out=f32(w_out).reshape(D, D), gain_b=gain_b,
                  gvec=np.ascontiguousarray(gvec), sinks_rep=sinks_rep, **cst)
    in_maps = [dict(x=np.ascontiguousarray(x[i]), **common) for i in range(NCORES)]
    res = run_bass_kernel_spmd(nc, in_maps, core_ids=list(range(NCORES)))
    return np.stack([np.asarray(r["out"], dtype=np.float32) for r in res.results], axis=0)
```

```python
import math
import numpy as np
import ml_dtypes
import concourse.bass as bass
import concourse.mybir as mybir
from concourse.bass_utils import run_bass_kernel_spmd

F32 = mybir.dt.float32
BF16 = mybir.dt.bfloat16
AF = mybir.ActivationFunctionType
ALU = mybir.AluOpType

S = 4096
D = 1024
E_IN = 3328
NCORES = 8
EPS = 1e-6
TB = 512
NTB = S // TB

C_QA, C_KA, C_VA, C_GA, C_QB, C_KB, C_VB, C_GB = 0, 512, 640, 768, 1280, 1792, 2304, 2816


class Res:
    __slots__ = ("name", "w", "r")

    def __init__(self, name):
        self.name = name
        self.w = {}
        self.r = {}


class Eng:
    def __init__(self, name, sem, is_pe=False, is_queue=False):
        self.name = name
        self.sem = sem
        self.count = 0
        self.ops = []
        self.waited = {}
        self.is_pe = is_pe
        self.is_queue = is_queue


class DmaSem:
    def __init__(self, sem):
        self.sem = sem
        self.count = 0


class Sched:
    def __init__(self, nc, ctx):
        self.nc = nc
        self.ctx = ctx
        mk = lambda n: ctx.enter_context(nc.semaphore(n))
        self.pe = Eng("pe", mk("s_pe"), is_pe=True)
        self.act = Eng("act", mk("s_act"))
        self.dve = Eng("dve", mk("s_dve"))
        self.pool = Eng("pool", mk("s_pool"))
        self.sp = Eng("sp", mk("s_sp"), is_queue=True)
        self.engs = [self.pe, self.act, self.dve, self.pool, self.sp]
        self.dsems = []
        self.dsem_of = {}

    def dma_sem(self, name):
        d = DmaSem(self.ctx.enter_context(self.nc.semaphore(name)))
        self.dsems.append(d)
        self.dsem_of[d.sem] = d
        return d

    def op(self, eng, fn, reads=(), writes=(), dsem=None, partial=False):
        waits = {}

        def need(d):
            for s, v in d.items():
                if waits.get(s, 0) < v:
                    waits[s] = v

        for R in reads:
            need(R.w)
        for W in writes:
            need(W.w)
            need(W.r)
        final = []
        for s, v in waits.items():
            if eng.is_pe and s is eng.sem:
                continue
            if s in self.dsem_of:
                v = self.dsem_of[s].count
            if eng.waited.get(s, 0) >= v:
                continue
            eng.waited[s] = v
            final.append((s, v))
        if dsem is None:
            eng.count += 1
            ev = (eng.sem, eng.count)
            inc = (eng.sem, 1)
        else:
            dsem.count += 16
            ev = (dsem.sem, dsem.count)
            inc = (dsem.sem, 16)
        eng.ops.append((final, fn, inc))
        for R in reads:
            if R.r.get(ev[0], 0) < ev[1]:
                R.r[ev[0]] = ev[1]
        for W in writes:
            if partial:
                W.w[ev[0]] = ev[1]
            else:
                W.w = {ev[0]: ev[1]}
                W.r = {}
        return ev

    def barrier(self):
        allev = {}
        for e in self.engs:
            if not e.is_queue and e.count:
                allev[e.sem] = e.count
        for d in self.dsems:
            if d.count:
                allev[d.sem] = d.count
        for e in self.engs:
            final = []
            for s, v in allev.items():
                if e.waited.get(s, 0) >= v:
                    continue
                e.waited[s] = v
                final.append((s, v))
            if final:
                e.ops.append((final, None, None))

    def flush(self):
        nc = self.nc
        with nc.Block() as block:
            def replay(eng):
                def run(e):
                    for waits, fn, inc in eng.ops:
                        for s, v in waits:
                            e.wait_ge(s, v)
                        if fn is not None:
                            fn(e).then_inc(inc[0], inc[1])
                return run

            block.tensor(replay(self.pe))
            block.scalar(replay(self.act))
            block.vector(replay(self.dve))
            block.gpsimd(replay(self.pool))
            block.sync(replay(self.sp))
        for e in self.engs:
            e.ops = []


def build_nc(debug=False, stop_after=3):
    from contextlib import ExitStack

    nc = bass.Bass("TRN2", target_bir_lowering=False)
    dt = lambda name, shape, dtype, kind: nc.dram_tensor(name, shape, dtype, kind=kind).ap()
    x = dt("x", [S, D], F32, "ExternalInput")
    w_in = dt("w_in", [D, E_IN], F32, "ExternalInput")
    w_out = dt("w_out", [D, D], F32, "ExternalInput")
    gain_b = dt("gain_b", [128, D], F32, "ExternalInput")
    gvec = dt("gvec", [128, 4], F32, "ExternalInput")
    sinks_rep = dt("sinks_rep", [128, 8], F32, "ExternalInput")
    lmask = dt("lmask", [128, 8], F32, "ExternalInput")
    cosT = dt("cosT", [128, S], F32, "ExternalInput")
    sinT = dt("sinT", [128, S], F32, "ExternalInput")
    masks = dt("masks", [5, 128, 1024], BF16, "ExternalInput")
    ident = dt("ident", [128, 128], BF16, "ExternalInput")
    bones = dt("bones", [128, 128], BF16, "ExternalInput")
    rmat = dt("rmat", [128, 128], BF16, "ExternalInput")
    out = dt("out", [S, D], F32, "ExternalOutput")
    sk = "ExternalOutput" if debug else "Internal"
    QTs = dt("QTs", [8, 128, S], BF16, sk)
    KTs = dt("KTs", [5, 128, S], BF16, sk)
    GTs = dt("GTs", [8, 128, S], BF16, sk)
    Vs = dt("Vs", [S, 12, 128], BF16, sk)
    MTs = dt("MTs", [8, 128, S], BF16, sk)

    with ExitStack() as ctx:
        sch = Sched(nc, ctx)
        op = sch.op
        PE, ACT, DVE, POOL, SP = sch.pe, sch.act, sch.dve, sch.pool, sch.sp
        sb = lambda name, shape, dtype, c=ctx: c.enter_context(nc.sbuf_tensor(name, shape, dtype))
        ps = lambda name, shape, dtype, c=ctx: c.enter_context(nc.psum_tensor(name, shape, dtype))

        r_QTs = [Res(f"QTs{i}") for i in range(8)]
        r_KTs = [Res(f"KTs{i}") for i in range(5)]
        r_GTs = [Res(f"GTs{i}") for i in range(8)]
        r_Vs = Res("Vs")
        r_MTs = [Res(f"MTs{i}") for i in range(8)]

        ident_sb = sb("ident_sb", [128, 128], BF16)
        r_ident = Res("ident")
        wo16 = sb("wo16", [128, 8, D], BF16)
        r_wo = Res("wo")
        d_wo = sch.dma_sem("d_wo")
        d_const = sch.dma_sem("d_const")
        op(SP, lambda e: e.dma_start(out=ident_sb[:], in_=ident), writes=[r_ident], dsem=d_const)

        with ExitStack() as c1:
            sb1 = lambda n, s, d: sb(n, s, d, c1)
            ps1 = lambda n, s, d: ps(n, s, d, c1)
            w16 = sb1("w16", [128, 8, E_IN], BF16)
            gainb = sb1("gainb", [128, D], F32)
            gv = sb1("gv", [128, 4], F32)
            bones_sb = sb1("bones_sb", [128, 128], BF16)
            rmat_sb = sb1("rmat_sb", [128, 128], BF16)
            NXT = 4
            xt = [sb1(f"xt{i}", [128, D], F32) for i in range(NXT)]
            xs16 = [sb1(f"xs16_{i}", [128, D], BF16) for i in range(4)]
            junk16 = sb1("junk16", [128, D], BF16)
            ss = sb1("ss", [128, 64], F32)
            lnss = sb1("lnss", [128, 64], F32)
            rstd = sb1("rstd", [128, 64], F32)
            xsT = [sb1(f"xsT{i}", [128, 8, TB], BF16) for i in range(2)]
            cosb = [sb1(f"cosb{i}", [128, TB], F32) for i in range(2)]
            sinb = [sb1(f"sinb{i}", [128, TB], F32) for i in range(2)]
            NCH = 4
            sq16 = [sb1(f"sq16_{i}", [128, TB], BF16) for i in range(NCH)]
            qc16 = [sb1(f"qc16_{i}", [128, TB], BF16) for i in range(NCH)]
            lnv = [sb1(f"lnv{i}", [128, TB], F32) for i in range(NCH)]
            rinv = [sb1(f"rinv{i}", [128, TB], F32) for i in range(NCH)]
            ta = [sb1(f"ta{i}", [128, TB], F32) for i in range(NCH)]
            tb_ = [sb1(f"tb{i}", [128, TB], F32) for i in range(NCH)]
            tc = [sb1(f"tc{i}", [128, TB], F32) for i in range(NCH)]
            NST = 4
            o16 = [sb1(f"o16_{i}", [128, TB], BF16) for i in range(NST)]
            g16 = [sb1(f"g16_{i}", [128, TB], BF16) for i in range(NST)]
            vst = [sb1(f"vst{i}", [128, 12, 128], BF16) for i in range(2)]

            tr_ps = ps1("tr_ps", [128, 8, 128], BF16)
            NPJ = 3
            pj_ps = [ps1(f"pj_ps{i}", [128, TB], F32) for i in range(NPJ)]
            ms_ps = [ps1(f"ms_ps{i}", [128, TB], F32) for i in range(2)]
            rot_ps = [ps1(f"rot_ps{i}", [128, TB], F32) for i in range(2)]

            R = lambda n: Res(n)
            r_w16 = [R(f"w16_{i}") for i in range(7)]
            r_gainb, r_gv, r_bones, r_rmat = R("gainb"), R("gv"), R("bones"), R("rmat")
            r_xt = [R(f"xt{i}") for i in range(NXT)]
            r_xs16 = [R(f"xs16{i}") for i in range(4)]
            r_junk = R("junk")
            r_ss, r_lnss, r_rstd = R("ss"), R("lnss"), R("rstd")
            r_xsT = [R(f"xsT{i}") for i in range(2)]
            r_cos = [R(f"cos{i}") for i in range(2)]
            r_sin = [R(f"sin{i}") for i in range(2)]
            r_sq = [R(f"sq{i}") for i in range(NCH)]
            r_qc = [R(f"qc{i}") for i in range(NCH)]
            r_lnv = [R(f"lnv{i}") for i in range(NCH)]
            r_rinv = [R(f"rinv{i}") for i in range(NCH)]
            r_ta = [R(f"ta{i}") for i in range(NCH)]
            r_tb = [R(f"tb{i}") for i in range(NCH)]
            r_tc = [R(f"tc{i}") for i in range(NCH)]
            r_o16 = [R(f"o16{i}") for i in range(NST)]
            r_g16 = [R(f"g16{i}") for i in range(NST)]
            r_vst = [R(f"vst{i}") for i in range(2)]
            r_tr = R("tr_ps")
            r_pj = [R(f"pj{i}") for i in range(NPJ)]
            r_ms = [R(f"ms{i}") for i in range(2)]
            r_rot = [R(f"rot{i}") for i in range(2)]

            d_w = [sch.dma_sem(f"d_w{i}") for i in range(7)]
            d_xt = [sch.dma_sem(f"d_xt{i}") for i in range(NXT)]
            d_cs = [sch.dma_sem(f"d_cs{i}") for i in range(2)]
            d_o16 = [sch.dma_sem(f"d_o16{i}") for i in range(NST)]
            d_g16 = [sch.dma_sem(f"d_g16{i}") for i in range(NST)]
            d_vst = [sch.dma_sem(f"d_vst{i}") for i in range(2)]

            op(SP, lambda e: e.dma_start(out=gainb[:], in_=gain_b), writes=[r_gainb], dsem=d_const)
            op(SP, lambda e: e.dma_start(out=gv[:], in_=gvec), writes=[r_gv], dsem=d_const)
            op(SP, lambda e: e.dma_start(out=bones_sb[:], in_=bones), writes=[r_bones], dsem=d_const)
            op(SP, lambda e: e.dma_start(out=rmat_sb[:], in_=rmat), writes=[r_rmat], dsem=d_const)
            for i in range(2):
                op(POOL, lambda e, i=i: e.memset(vst[i][:], 1.0), writes=[r_vst[i]])
            w_view = w_in.rearrange("(c p) e -> p c e", p=128)
            wblocks = [(0, 512), (512, 1024), (1280, 1792), (1792, 2304), (2304, 2816), (1024, 1280), (2816, 3328)]
            wblk_of_col = {}
            for bi, (c0, c1_) in enumerate(wblocks):
                for cc in range(c0, c1_, 128):
                    wblk_of_col[cc] = bi
                op(POOL, lambda e, c0=c0, c1_=c1_: e.dma_start(out=w16[:, :, c0:c1_], in_=w_view[:, :, c0:c1_]),
                   writes=[r_w16[bi]], dsem=d_w[bi])
            op(POOL, lambda e: e.dma_start(out=wo16[:], in_=w_out.rearrange("(c p) e -> p c e", p=128)),
               writes=[r_wo], dsem=d_wo)


            def xload(n, j):
                bs = n % 2
                k = n * 4 + j
                sl = k % NXT
                row0 = n * TB + j * 128
                op(SP, lambda e: e.dma_start(out=xt[sl][:], in_=x[row0:row0 + 128, :]),
                   writes=[r_xt[sl]], dsem=d_xt[sl])
                if j == 0:
                    op(SP, lambda e: e.dma_start(out=cosb[bs][:], in_=cosT[:, n * TB:(n + 1) * TB]),
                       writes=[r_cos[bs]], dsem=d_cs[bs])
                    op(SP, lambda e: e.dma_start(out=sinb[bs][:], in_=sinT[:, n * TB:(n + 1) * TB]),
                       writes=[r_sin[bs]], dsem=d_cs[bs])

            def xprep1(n, j):
                bs = n % 2
                k = n * 4 + j
                sl = k % NXT
                s2 = k % 4
                col = k % 64
                op(ACT, lambda e: e.activation(out=junk16[:], in_=xt[sl][:], func=AF.Square,
                                               accum_out=ss[:, col:col + 1]),
                   reads=[r_xt[sl]], writes=[r_junk, r_ss], partial=True)
                op(ACT, lambda e: e.activation(out=lnss[:, col:col + 1], in_=ss[:, col:col + 1], func=AF.Ln,
                                               scale=1.0 / D, bias=EPS),
                   reads=[r_ss], writes=[r_lnss], partial=True)
                op(ACT, lambda e: e.activation(out=rstd[:, col:col + 1], in_=lnss[:, col:col + 1], func=AF.Exp,
                                               scale=-0.5),
                   reads=[r_lnss], writes=[r_rstd], partial=True)
                op(DVE, lambda e: e.scalar_tensor_tensor(
                    out=xs16[s2][:], in0=xt[sl][:], scalar=rstd[:, col:col + 1], in1=gainb[:],
                    op0=ALU.mult, op1=ALU.mult),
                   reads=[r_xt[sl], r_rstd, r_gainb], writes=[r_xs16[s2]])

            def xprep2(n, j):
                bs = n % 2
                s2 = (n * 4 + j) % 4
                for c in range(8):
                    op(PE, lambda e, c=c: e.transpose(out=tr_ps[:, c, :], in_=xs16[s2][:, c * 128:(c + 1) * 128],
                                                      identity=ident_sb[:]),
                       reads=[r_xs16[s2], r_ident], writes=[r_tr], partial=(c > 0))
                op(ACT, lambda e: e.activation(out=xsT[bs][:, :, j * 128:(j + 1) * 128], in_=tr_ps[:], func=AF.Copy),
                   reads=[r_tr], writes=[r_xsT[bs]], partial=(j > 0))

            pj_ctr = [0]
            ch_ctr = [0]
            st_ctr = [0]
            gst_ctr = [0]
            vst_ctr = [0]

            def proj_fm(n, col0):
                bs = n % 2
                k = pj_ctr[0]
                pj_ctr[0] += 1
                pslot = k % NPJ
                for dc in range(8):
                    op(PE, lambda e, pslot=pslot, dc=dc, bs=bs: e.matmul(
                        pj_ps[pslot][:], lhsT=w16[:, dc, col0:col0 + 128], rhs=xsT[bs][:, dc, :],
                        start=(dc == 0), stop=(dc == 7)),
                       reads=[r_w16[wblk_of_col[col0]], r_xsT[bs]], writes=[r_pj[pslot]], partial=(dc > 0))
                return pslot

            def qk_chain_a(n, pslot, gcol):
                k = ch_ctr[0]
                ch_ctr[0] += 1
                cs = k % NCH
                m2 = k % 2
                op(ACT, lambda e: e.activation(out=sq16[cs][:], in_=pj_ps[pslot][:], func=AF.Square),
                   reads=[r_pj[pslot]], writes=[r_sq[cs]])
                op(ACT, lambda e: e.activation(out=qc16[cs][:], in_=pj_ps[pslot][:], func=AF.Identity,
                                               scale=gv[:, gcol:gcol + 1]),
                   reads=[r_pj[pslot], r_gv], writes=[r_qc[cs]])
                return (k, cs, m2)

            def qk_chain_pe(state):
                k, cs, m2 = state
                op(PE, lambda e: e.matmul(ms_ps[m2][:], lhsT=bones_sb[:], rhs=sq16[cs][:], start=True, stop=True),
                   reads=[r_bones, r_sq[cs]], writes=[r_ms[m2]])
                op(PE, lambda e: e.matmul(rot_ps[m2][:], lhsT=rmat_sb[:], rhs=qc16[cs][:], start=True, stop=True),
                   reads=[r_rmat, r_qc[cs]], writes=[r_rot[m2]])

            def qk_chain_b1(n, pslot, gcol, state):
                k, cs, m2 = state
                bs = n % 2
                op(DVE, lambda e: e.scalar_tensor_tensor(out=ta[cs][:], in0=pj_ps[pslot][:], scalar=gv[:, gcol:gcol + 1],
                                                         in1=cosb[bs][:], op0=ALU.mult, op1=ALU.mult),
                   reads=[r_pj[pslot], r_gv, r_cos[bs], r_qc[cs]], writes=[r_ta[cs]])
                op(DVE, lambda e: e.tensor_tensor(out=tb_[cs][:], in0=rot_ps[m2][:], in1=sinb[bs][:], op=ALU.mult),
                   reads=[r_rot[m2], r_sin[bs]], writes=[r_tb[cs]])
                op(POOL, lambda e: e.tensor_tensor(out=tc[cs][:], in0=ta[cs][:], in1=tb_[cs][:], op=ALU.add),
                   reads=[r_ta[cs], r_tb[cs]], writes=[r_tc[cs]])

            def qk_chain_b2(n, is_q, state, dst, r_dst):
                k, cs, m2 = state
                op(ACT, lambda e: e.activation(out=lnv[cs][:], in_=ms_ps[m2][:], func=AF.Ln, bias=EPS),
                   reads=[r_ms[m2]], writes=[r_lnv[cs]])
                op(ACT, lambda e: e.activation(out=rinv[cs][:], in_=lnv[cs][:], func=AF.Exp, scale=-0.5,
                                               bias=(-math.log(8.0) if is_q else 0.0)),
                   reads=[r_lnv[cs]], writes=[r_rinv[cs]])
                s = st_ctr[0] % NST
                st_ctr[0] += 1
                op(DVE, lambda e: e.tensor_tensor(out=o16[s][:], in0=tc[cs][:], in1=rinv[cs][:], op=ALU.mult),
                   reads=[r_tc[cs], r_rinv[cs]], writes=[r_o16[s]])
                op(SP, lambda e: e.dma_start(out=dst[:, n * TB:(n + 1) * TB], in_=o16[s][:]),
                   reads=[r_o16[s]], writes=[r_dst], dsem=d_o16[s], partial=True)

            def gate_chunk(n, pslot, dst, r_dst):
                s = gst_ctr[0] % NST
                gst_ctr[0] += 1
                op(ACT, lambda e: e.activation(out=g16[s][:], in_=pj_ps[pslot][:], func=AF.Silu),
                   reads=[r_pj[pslot]], writes=[r_g16[s]])
                op(SP, lambda e: e.dma_start(out=dst[:, n * TB:(n + 1) * TB], in_=g16[s][:]),
                   reads=[r_g16[s]], writes=[r_dst], dsem=d_g16[s], partial=True)

            def v_tile(n, j):
                bs = n % 2
                k = pj_ctr[0]
                pj_ctr[0] += 2
                pa, pb = k % NPJ, (k + 1) % NPJ
                for dc in range(8):
                    op(PE, lambda e, dc=dc: e.matmul(pj_ps[pb][:], lhsT=xsT[bs][:, dc, j * 128:(j + 1) * 128],
                                                     rhs=w16[:, dc, C_VB:C_VB + 512], start=(dc == 0), stop=(dc == 7)),
                       reads=[r_w16[wblk_of_col[C_VB]], r_xsT[bs]], writes=[r_pj[pb]], partial=(dc > 0))
                for dc in range(8):
                    op(PE, lambda e, dc=dc: e.matmul(pj_ps[pa][:, 0:128], lhsT=xsT[bs][:, dc, j * 128:(j + 1) * 128],
                                                     rhs=w16[:, dc, C_VA:C_VA + 128], start=(dc == 0), stop=(dc == 7)),
                       reads=[r_w16[wblk_of_col[C_VA]], r_xsT[bs]], writes=[r_pj[pa]], partial=(dc > 0))
                vs = vst_ctr[0] % 2
                vst_ctr[0] += 1
                vv = vst[vs][:].rearrange("p (m two) (h e) -> p m (two h) e", two=2, h=2)
                op(DVE, lambda e: e.tensor_copy(out=vv[:, 2:6, 0:4:3, :],
                                                in_=pj_ps[pb][:].rearrange("p (m two e) -> p m two e", two=2, e=64)),
                   reads=[r_pj[pb]], writes=[r_vst[vs]])
                va_in = pj_ps[pa][:, 0:128].rearrange("p (k e) -> p k e", e=64)
                op(DVE, lambda e: e.tensor_copy(out=vv[:, 0:2, 0, :], in_=va_in),
                   reads=[r_pj[pa]], writes=[r_vst[vs]], partial=True)
                op(DVE, lambda e: e.tensor_copy(out=vv[:, 0:2, 3, :], in_=va_in),
                   reads=[r_pj[pa]], writes=[r_vst[vs]], partial=True)
                row0 = n * TB + j * 128
                op(SP, lambda e: e.dma_start(out=Vs[row0:row0 + 128, :, :], in_=vst[vs][:]),
                   reads=[r_vst[vs]], writes=[r_Vs], dsem=d_vst[vs], partial=True)

            chunks = []
            for i in range(4):
                chunks.append(("q", C_QA + 128 * i, 0, QTs[i], r_QTs[i]))
            chunks.append(("k", C_KA, 1, KTs[0], r_KTs[0]))
            for i in range(4):
                chunks.append(("q", C_QB + 128 * i, 2, QTs[4 + i], r_QTs[4 + i]))
            for i in range(4):
                chunks.append(("k", C_KB + 128 * i, 3, KTs[1 + i], r_KTs[1 + i]))
            gchunks = []
            for i in range(4):
                gchunks.append((C_GA + 128 * i, GTs[i], r_GTs[i]))
            for i in range(4):
                gchunks.append((C_GB + 128 * i, GTs[4 + i], r_GTs[4 + i]))

            for j in range(4):
                xload(0, j)
            for j in range(4):
                xprep1(0, j)
            for j in range(4):
                xprep2(0, j)
            NQK = len(chunks)
            for n in range(NTB):
                st8 = {}

                def s1(i):
                    kind, col0, gcol, dst, r_dst = chunks[i]
                    pslot = proj_fm(n, col0)
                    st8[i] = (pslot, qk_chain_a(n, pslot, gcol))

                def s2(i):
                    pslot, state = st8[i]
                    qk_chain_pe(state)
                    qk_chain_b1(n, pslot, chunks[i][2], state)

                def s3(i):
                    pslot, state = st8[i]
                    qk_chain_b2(n, chunks[i][0] == "q", state, chunks[i][3], chunks[i][4])

                if n + 1 < NTB:
                    for j in range(4):
                        xload(n + 1, j)
                for i in range(NQK):
                    s1(i)
                    if i >= 1:
                        s2(i - 1)
                    if i >= 2:
                        s3(i - 2)
                    if n + 1 < NTB and i in (1, 4, 7, 10):
                        xprep1(n + 1, (i - 1) // 3)
                v_tile(n, 0)
                s2(NQK - 1)
                v_tile(n, 1)
                s3(NQK - 2)
                v_tile(n, 2)
                s3(NQK - 1)
                v_tile(n, 3)
                for gi, (col0, dst, r_dst) in enumerate(gchunks):
                    pslot = proj_fm(n, col0)
                    gate_chunk(n, pslot, dst, r_dst)
                    if n + 1 < NTB and gi < 4:
                        xprep2(n + 1, gi)
            sch.barrier()
            sch.flush()

        if stop_after >= 2:
            with ExitStack() as c2:
                sb2 = lambda n, s, d: sb(n, s, d, c2)
                ps2 = lambda n, s, d: ps(n, s, d, c2)
                mask_sb = sb2("mask_sb", [128, 5, 1024], BF16)
                sinkb = sb2("sinkb", [128, 8], F32)
                lmask_sb = sb2("lmask_sb", [128, 8], F32)
                QTc = [sb2(f"QTc{i}", [128, S], BF16) for i in range(2)]
                KTc = [sb2(f"KTc{i}", [128, S], BF16) for i in range(2)]
                Gc = [sb2(f"Gc{i}", [128, S], BF16) for i in range(2)]
                NV = 4
                Vt = [sb2(f"Vt{i}", [128, 32, 256], BF16) for i in range(NV)]
                acc = [sb2(f"acc{i}", [128, S], F32) for i in range(2)]
                Rt = sb2("Rt", [128, 4, 512], F32)
                Mx = sb2("Mx", [128, 4, 512], BF16)
                QTf = {4: sb2("QTf4", [128, S], BF16), 16: sb2("QTf16", [128, S], BF16)}
                r_QTf = {4: Res("QTf4"), 16: Res("QTf16")}
                NP = 4
                Pb = [sb2(f"Pb{i}", [128, 1024], BF16) for i in range(NP)]
                NSTB = 2
                st_ps = [ps2(f"st_ps{i}", [128, 1024], F32) for i in range(NSTB)]
                ot_ps = [ps2(f"ot_ps{i}", [128, 512], F32) for i in range(2)]
                dm_ps = ps2("dm_ps", [128, 512], F32)
                r_dm = Res("dm")
                NFILL = 1

                R = lambda n: Res(n)
                r_mask, r_sinkb, r_lmask = R("mask"), R("sinkb"), R("lmask")
                r_QTc = [R(f"QTc{i}") for i in range(2)]
                r_KTc = [R(f"KTc{i}") for i in range(2)]
                r_Gc = [R(f"Gc{i}") for i in range(2)]
                r_Vt = [R(f"Vt{i}") for i in range(NV)]
                r_acc = [R(f"acc{i}") for i in range(2)]
                r_Rt = R("Rt")
                r_Mx = [R(f"Mx{i}") for i in range(4)]
                r_Pb = [R(f"Pb{i}") for i in range(NP)]
                r_st = [R(f"st{i}") for i in range(NSTB)]
                r_ot = [R(f"ot{i}") for i in range(2)]
                d_c2 = sch.dma_sem("d_c2")
                d_q = [sch.dma_sem(f"d_q{i}") for i in range(2)]
                d_k = [sch.dma_sem(f"d_k{i}") for i in range(2)]
                d_g = [sch.dma_sem(f"d_g{i}") for i in range(2)]
                d_v = [sch.dma_sem(f"d_v{i}") for i in range(NV)]
                d_mx = [sch.dma_sem(f"d_mx{i}") for i in range(4)]

                op(SP, lambda e: e.dma_start(out=mask_sb[:], in_=masks.rearrange("m p f -> p m f")),
                   writes=[r_mask], dsem=d_c2)
                op(SP, lambda e: e.dma_start(out=sinkb[:], in_=sinks_rep), writes=[r_sinkb], dsem=d_c2)
                op(SP, lambda e: e.dma_start(out=lmask_sb[:], in_=lmask), writes=[r_lmask], dsem=d_c2)
                op(ACT, lambda e: e.activation(out=sinkb[:], in_=sinkb[:], func=AF.Exp), reads=[], writes=[r_sinkb])
                op(DVE, lambda e: e.tensor_tensor(out=sinkb[:], in0=sinkb[:], in1=lmask_sb[:], op=ALU.mult),
                   reads=[r_lmask], writes=[r_sinkb])

                Vs_flat = Vs.rearrange("t s e -> t (s e)")
                bt_ctr = [0]

                def load_pair(c):
                    sl = c % 2
                    op(SP, lambda e: e.dma_start(out=QTc[sl][:], in_=QTs[c]), reads=[r_QTs[c]], writes=[r_QTc[sl]],
                       dsem=d_q[sl])
                    if c < 4:
                        kv = c // 2
                        op(SP, lambda e: e.dma_start(out=KTc[sl][0:64, :], in_=KTs[0][64 * kv:64 * kv + 64, :]),
                           reads=[r_KTs[0]], writes=[r_KTc[sl]], dsem=d_k[sl])
                        op(SP, lambda e: e.dma_start(out=KTc[sl][64:128, :], in_=KTs[0][64 * kv:64 * kv + 64, :]),
                           reads=[r_KTs[0]], writes=[r_KTc[sl]], dsem=d_k[sl], partial=True)
                    else:
                        op(SP, lambda e: e.dma_start(out=KTc[sl][:], in_=KTs[1 + (c - 4)]), reads=[r_KTs[1 + (c - 4)]],
                           writes=[r_KTc[sl]], dsem=d_k[sl])

                def load_G(c):
                    sl = c % 2
                    op(SP, lambda e: e.dma_start(out=Gc[sl][:], in_=GTs[c]), reads=[r_GTs[c]], writes=[r_Gc[sl]],
                       dsem=d_g[sl])

                def load_v(c, dil):
                    vs_ = {1: (0 if c % 2 == 0 else 3), 4: 1, 16: 2}[dil]
                    slot0 = 2 * (c // 2) if c < 4 else 4 + 2 * (c - 4)
                    f0 = slot0 * 128
                    nb = 32 // dil
                    first = True
                    for r in range(dil):
                        src = Vs_flat.rearrange("(b i r) f -> r i b f", i=128, r=dil)[r][:, :, f0:f0 + 256]
                        op(SP, lambda e, src=src, r=r: e.dma_start(out=Vt[vs_][:, r * nb:(r + 1) * nb, :], in_=src),
                           reads=[r_Vs], writes=[r_Vt[vs_]], dsem=d_v[vs_], partial=not first)
                        first = False
                    return vs_

                NBLK = S // 512
                r_accb = [[Res(f"acc{h}_{j}") for j in range(NBLK)] for h in range(2)]

                def make_job(c, sl, hl, dil, vs_, is_a, first_pat, bt):
                    nb = 32 // dil
                    k = bt_ctr[0]
                    bt_ctr[0] += 1
                    units = [4 * bt + i for i in range(4)]
                    firsts = [(u % nb) == 0 for u in units]
                    if is_a:
                        mi = 4 if firsts[0] else 3
                    elif firsts[0] and firsts[2]:
                        mi = 2
                    elif firsts[0]:
                        mi = 1
                    else:
                        mi = 0
                    return dict(c=c, sl=sl, hl=hl, dil=dil, vs_=vs_, is_a=is_a, first_pat=first_pat, bt=bt, nb=nb,
                                p0=64 * hl, stb=k % NSTB, pb=k % NP, ob=k % 2, units=units, firsts=firsts, mi=mi)

                def stage_a(J):
                    sl, dil, nb, p0, stb, pb, mi = J["sl"], J["dil"], J["nb"], J["p0"], J["stb"], J["pb"], J["mi"]

                    def tok(u):
                        r, b = u // nb, u % nb
                        start = r + dil * 128 * b
                        return slice(start, start + dil * 127 + 1, dil)

                    units, firsts = J["units"], J["firsts"]

                    def tok2(u):
                        r, b = u // nb, u % nb
                        start = r + dil * 128 * b
                        return slice(start, start + dil * 255 + 1, dil)

                    def qk(kunit, qunit, pos, n, first):
                        if dil == 1:
                            start = 128 * qunit
                            rhs = QTc[sl][p0:p0 + 64, start:start + n]
                            rq = r_QTc[sl]
                        else:
                            r, b = qunit // nb, qunit % nb
                            start = r * (S // dil) + 128 * b
                            rhs = QTf[dil][p0:p0 + 64, start:start + n]
                            rq = r_QTf[dil]
                        op(PE, lambda e: e.matmul(
                            st_ps[stb][:, 128 * pos:128 * pos + n], lhsT=KTc[sl][p0:p0 + 64, tok(kunit)],
                            rhs=rhs, start=True, stop=True),
                           reads=[r_KTc[sl], rq], writes=[r_st[stb]], partial=not first)

                    u0 = units[0]
                    qk(u0 if u0 == 0 else u0 - 1, u0, 7, 128, True)
                    for i in range(3):
                        if not firsts[i + 1]:
                            qk(units[i], units[i], 2 * i, 256, False)
                        else:
                            qk(units[i], units[i], 2 * i, 128, False)
                            qk(units[i], units[i + 1], 2 * i + 1, 128, False)
                    qk(units[3], units[3], 6, 128, False)
                    for _ in range(NFILL):
                        op(PE, lambda e: e.matmul(dm_ps[:], lhsT=mask_sb[:, 0, 0:128], rhs=mask_sb[:, 1, 0:512],
                                                  start=True, stop=True),
                           reads=[r_mask], writes=[r_dm])
                    op(ACT, lambda e: e.activation(out=Pb[pb][:], in_=st_ps[stb][:], func=AF.Exp),
                       reads=[r_st[stb]], writes=[r_Pb[pb]])
                    op(DVE, lambda e: e.tensor_tensor(out=Pb[pb][:], in0=Pb[pb][:], in1=mask_sb[:, mi, :], op=ALU.mult),
                       reads=[r_mask], writes=[r_Pb[pb]])

                def stage_b(J):
                    c, hl, dil, nb, pb, ob, vs_ = J["c"], J["hl"], J["dil"], J["nb"], J["pb"], J["ob"], J["vs_"]
                    units = J["units"]
                    a_t = acc[hl]
                    def pv(vunit, pos, col, n, start, stop, first):
                        op(PE, lambda e: e.matmul(
                            ot_ps[ob][:, col:col + n], lhsT=Vt[vs_][:, vunit, 128 * hl:128 * hl + 128],
                            rhs=Pb[pb][:, 128 * pos:128 * pos + n], start=start, stop=stop, skip_group_check=True),
                           reads=[r_Vt[vs_], r_Pb[pb]], writes=[r_ot[ob]], partial=not first)

                    u0 = units[0]
                    pv(u0 if u0 == 0 else u0 - 1, 7, 0, 128, True, False, True)
                    for i in range(3):
                        pv(units[i], 2 * i, 128 * i, 256, False, False, False)
                    pv(units[3], 6, 384, 128, False, True, False)
                    if dil == 16:
                        r0 = units[0] // nb
                        dst = a_t[:].rearrange("p (j r) -> p r j", r=16)[:, r0:r0 + 2, :]
                        src = ot_ps[ob][:].rearrange("p (r j) -> p r j", r=2)
                        blks = list(range(NBLK))
                    else:
                        r, b0 = units[0] // nb, units[0] % nb
                        start = r + dil * 128 * b0
                        dst = a_t[:, start:start + dil * 511 + 1:dil]
                        src = ot_ps[ob][:]
                        blks = list(range(start // 512, (start + dil * 511) // 512 + 1))
                    wr = [r_accb[hl][j] for j in blks]
                    if J["first_pat"]:
                        if J["is_a"]:
                            h = 2 * c + hl
                            op(DVE, lambda e: e.tensor_scalar(out=dst, in0=src, scalar1=sinkb[:, h:h + 1], scalar2=None,
                                                              op0=ALU.add),
                               reads=[r_ot[ob], r_sinkb], writes=wr, partial=True)
                        else:
                            op(DVE, lambda e: e.tensor_copy(out=dst, in_=src),
                               reads=[r_ot[ob]], writes=wr, partial=True)
                    else:
                        op(DVE, lambda e: e.tensor_tensor(out=dst, in0=src, in1=dst, op=ALU.add),
                           reads=[r_ot[ob]], writes=wr, partial=True)

                def fin_act(c, sl, j):
                    cs_ = slice(512 * j, 512 * j + 512)
                    jb = j % 4
                    rR = r_Rtb[jb]
                    op(ACT, lambda e: e.activation(out=Rt[0:64, jb, :], in_=acc[0][64:128, cs_], func=AF.Ln),
                       reads=[r_accb[0][j]], writes=[rR])
                    op(ACT, lambda e: e.activation(out=Rt[64:128, jb, :], in_=acc[1][0:64, cs_], func=AF.Ln),
                       reads=[r_accb[1][j]], writes=[rR], partial=True)
                    op(ACT, lambda e: e.activation(out=Rt[:, jb, :], in_=Rt[:, jb, :], func=AF.Exp, scale=-1.0),
                       reads=[], writes=[rR])

                def fin_pool(c, sl, j):
                    cs_ = slice(512 * j, 512 * j + 512)
                    jb = j % 4
                    op(POOL, lambda e: e.tensor_tensor(out=Rt[:, jb, :], in0=Rt[:, jb, :], in1=Gc[sl][:, cs_], op=ALU.mult),
                       reads=[r_Gc[sl]], writes=[r_Rtb[jb]])

                def fin_dve(c, sl, j):
                    cs_ = slice(512 * j, 512 * j + 512)
                    jb = j % 4
                    rR = r_Rtb[jb]
                    op(DVE, lambda e: e.tensor_tensor(out=Mx[0:64, jb, :], in0=acc[0][0:64, cs_], in1=Rt[0:64, jb, :],
                                                      op=ALU.mult),
                       reads=[r_accb[0][j], rR], writes=[r_Mx[jb]])
                    op(DVE, lambda e: e.tensor_tensor(out=Mx[64:128, jb, :], in0=acc[1][64:128, cs_],
                                                      in1=Rt[64:128, jb, :], op=ALU.mult),
                       reads=[r_accb[1][j], rR], writes=[r_Mx[jb]], partial=True)

                def fin_store(c, sl, j):
                    cs_ = slice(512 * j, 512 * j + 512)
                    jb = j % 4
                    op(ACT, lambda e: e.dma_start(out=MTs[c][:, cs_], in_=Mx[:, jb, :]),
                       reads=[r_Mx[jb]], writes=[r_MTs[c]], dsem=d_mx[jb], partial=(j > 0))

                FIN_STAGES = [fin_act, fin_pool, fin_dve, fin_store]

                act_bg = []

                def deinterleave(c, dil, spread=False):
                    sl = c % 2
                    L = S // dil
                    nparts = 4
                    rr = dil // nparts
                    if dil == 16 and spread:
                        src = QTf[4][:].rearrange("p (r i a) -> p a r i", r=4, a=4)
                        dst = QTf[16][:].rearrange("p (a r i) -> p a r i", a=4, r=4)
                        rsrc = r_QTf[4]

                        def piece(q):
                            op(ACT, lambda e: e.activation(out=dst[:, q, :, :], in_=src[:, q, :, :], func=AF.Copy),
                               reads=[rsrc], writes=[r_QTf[dil]], partial=(q > 0))
                    else:
                        src = QTc[sl][:].rearrange("p (i r) -> p r i", r=dil)
                        dst = QTf[dil][:].rearrange("p (r i) -> p r i", r=dil)

                        def piece(q):
                            op(ACT, lambda e: e.activation(out=dst[:, q * rr:(q + 1) * rr, :],
                                                           in_=src[:, q * rr:(q + 1) * rr, :], func=AF.Copy),
                               reads=[r_QTc[sl]], writes=[r_QTf[dil]], partial=(q > 0))

                    for q in range(nparts):
                        if spread:
                            act_bg.append(((c, dil), lambda q=q: piece(q)))
                        else:
                            piece(q)

                r_Rtb = [Res(f"Rt{i}") for i in range(4)]

                def dils_of(c):
                    return [1] if c < 4 else [1, 4, 16]

                load_pair(0)
                load_G(0)
                load_G(1)
                vslot = {}
                for d in dils_of(0):
                    vslot[(0, d)] = load_v(0, d)
                seq = []
                for c in range(8):
                    sl = c % 2
                    is_a = c < 4
                    dils = dils_of(c)
                    if c + 1 < 8:
                        def pre(c=c):
                            load_pair(c + 1)
                            vslot[(c + 1, 1)] = load_v(c + 1, 1)
                        seq.append(("pre", pre))
                    for hl in range(2):
                        for pi, dil in enumerate(dils):
                            for bt in range(8):
                                seq.append(("job", (c, sl, hl, dil, is_a, pi == 0, bt)))
                            if hl == 1 and dil > 1 and c + 1 < 8:
                                def post_v(c=c, dil=dil):
                                    vslot[(c + 1, dil)] = load_v(c + 1, dil)
                                    deinterleave(c + 1, dil, spread=True)
                                seq.append(("post_deferred", post_v))
                    if c == 3:
                        def post3():
                            for d in (4, 16):
                                vslot[(4, d)] = load_v(4, d)
                                deinterleave(4, d, spread=True)
                        seq.append(("post_deferred", post3))
                    seq.append(("fin", (c, sl)))
                LAG = 2
                pending = []

                fin_state = {"c": None, "sl": None, "t": 0}

                def fin_active():
                    return fin_state["c"] is not None

                def fin_step():
                    c_, sl_, t = fin_state["c"], fin_state["sl"], fin_state["t"]
                    for st_i in (3, 2, 1, 0):
                        j_ = t - st_i
                        if 0 <= j_ < NBLK:
                            FIN_STAGES[st_i](c_, sl_, j_)
                    fin_state["t"] = t + 1
                    if t + 1 >= NBLK + 3:
                        fin_state["c"] = None
                        if c_ + 2 < 8:
                            load_G(c_ + 2)

                deferred = []
                jcount = [0]

                def do_stage_b(J):
                    stage_b(J)
                    while deferred and deferred[0][0] <= J["idx"]:
                        deferred.pop(0)[1]()

                def drain():
                    while pending:
                        do_stage_b(pending.pop(0))
                    while deferred:
                        deferred.pop(0)[1]()

                for kind, item in seq:
                    if kind == "pre":
                        item()
                    elif kind == "job":
                        c, sl, hl, dil, is_a, fp, bt = item
                        while any(tag == (c, dil) for tag, _ in act_bg):
                            act_bg.pop(0)[1]()
                        J = make_job(c, sl, hl, dil, vslot[(c, dil)], is_a, fp, bt)
                        J["idx"] = jcount[0]
                        jcount[0] += 1
                        stage_a(J)
                        pending.append(J)
                        if act_bg and (not fin_active() or fin_state["t"] >= NBLK):
                            act_bg.pop(0)[1]()
                        if fin_active():
                            fin_step()
                        if len(pending) > LAG:
                            do_stage_b(pending.pop(0))
                    elif kind == "post_deferred":
                        deferred.append((jcount[0] - 1, item))
                    elif kind == "post":
                        drain()
                        item()
                    else:
                        drain()
                        while fin_active():
                            fin_step()
                        c, sl = item
                        fin_state.update(c=c, sl=sl, t=0)
                drain()
                while act_bg:
                    act_bg.pop(0)[1]()
                while fin_active():
                    fin_step()
                sch.barrier()
                sch.flush()

        if stop_after >= 3:
            with ExitStack() as c3:
                sb3 = lambda n, s, d: sb(n, s, d, c3)
                ps3 = lambda n, s, d: ps(n, s, d, c3)
                mt = [sb3(f"mt{i}", [128, 8, TB], BF16) for i in range(2)]
                NX3 = 4
                x3 = [sb3(f"x3_{i}", [128, D], F32) for i in range(NX3)]
                o3 = [sb3(f"o3_{i}", [128, D], F32) for i in range(NX3)]
                op_ps = [ps3(f"op_ps{i}", [128, D], F32) for i in range(3)]
                R = lambda n: Res(n)
                r_mt = [R(f"mt{i}") for i in range(2)]
                r_x3 = [R(f"x3{i}") for i in range(NX3)]
                r_o3 = [R(f"o3{i}") for i in range(NX3)]
                r_op = [R(f"op{i}") for i in range(3)]
                d_mt = [sch.dma_sem(f"d_mt{i}") for i in range(2)]
                d_x3 = [sch.dma_sem(f"d_x3{i}") for i in range(NX3)]
                d_o3 = [sch.dma_sem(f"d_o3{i}") for i in range(NX3)]
                MT_v = MTs.rearrange("c p t -> p c t")
                def ld_mt(n):
                    ms_ = n % 2
                    op(SP, lambda e: e.dma_start(out=mt[ms_][:], in_=MT_v[:, :, n * TB:(n + 1) * TB]),
                       reads=r_MTs, writes=[r_mt[ms_]], dsem=d_mt[ms_])

                def ld_x(k):
                    xs_ = k % NX3
                    row0 = k * 128
                    op(SP, lambda e: e.dma_start(out=x3[xs_][:], in_=x[row0:row0 + 128, :]),
                       writes=[r_x3[xs_]], dsem=d_x3[xs_])

                ld_mt(0)
                for k in range(NX3):
                    ld_x(k)
                for n in range(NTB):
                    ms_ = n % 2
                    if n + 1 < NTB:
                        ld_mt(n + 1)
                    for j in range(4):
                        k3 = n * 4 + j
                        xs_ = k3 % NX3
                        pb = k3 % 3
                        row0 = k3 * 128
                        for half in range(2):
                            for cc in range(8):
                                op(PE, lambda e, pb=pb, half=half, cc=cc, ms_=ms_, j=j: e.matmul(
                                    op_ps[pb][:, 512 * half:512 * half + 512], lhsT=mt[ms_][:, cc, j * 128:(j + 1) * 128],
                                    rhs=wo16[:, cc, 512 * half:512 * half + 512], start=(cc == 0), stop=(cc == 7)),
                                   reads=[r_mt[ms_], r_wo], writes=[r_op[pb]], partial=(half > 0 or cc > 0))
                        op(DVE, lambda e, pb=pb, xs_=xs_: e.tensor_tensor(out=o3[xs_][:], in0=op_ps[pb][:], in1=x3[xs_][:],
                                                                          op=ALU.add),
                           reads=[r_op[pb], r_x3[xs_]], writes=[r_o3[xs_]])
                        op(SP, lambda e, xs_=xs_, row0=row0: e.dma_start(out=out[row0:row0 + 128, :], in_=o3[xs_][:]),
                           reads=[r_o3[xs_]], dsem=d_o3[xs_])
                        if k3 + NX3 < 4 * NTB:
                            ld_x(k3 + NX3)
                sch.barrier()
                sch.flush()
    return nc


_NC_CACHE = {}


def _consts():
    half = 32
    inv = 10000.0 ** (-np.arange(half, dtype=np.float64) / half)
    ang = np.arange(S, dtype=np.float64)[:, None] * inv[None, :]
    cos = np.cos(ang).astype(np.float32)
    sin = np.sin(ang).astype(np.float32)
    idx = np.arange(128) % 32
    cosT = np.ascontiguousarray(cos[:, idx].T)
    sinT = np.ascontiguousarray(sin[:, idx].T)
    bf = ml_dtypes.bfloat16
    kk = np.arange(128)[:, None]
    qq = np.arange(128)[None, :]
    prev_b = (kk >= qq).astype(np.float32)
    prev_a = (kk > qq).astype(np.float32)
    diag = (kk <= qq).astype(np.float32)
    zero = np.zeros((128, 128), np.float32)

    def mk(prev, firsts):
        pm = lambda i: zero if i in firsts else prev
        parts = [diag, pm(1), diag, pm(2), diag, pm(3), diag, pm(0)]
        return np.concatenate(parts, axis=1)

    masks = np.stack([mk(prev_b, ()), mk(prev_b, (0,)), mk(prev_b, (0, 2)), mk(prev_a, ()), mk(prev_a, (0,))]).astype(bf)
    ident = np.eye(128, dtype=np.float32).astype(bf)
    head = np.arange(128) // 64
    bones = ((head[:, None] == head[None, :]).astype(np.float32) / 64.0).astype(bf)
    rmat = np.zeros((128, 128), np.float32)
    for do in range(128):
        h, d = do // 64, do % 64
        if d < 32:
            rmat[h * 64 + d + 32, do] = -1.0
        else:
            rmat[h * 64 + d - 32, do] = 1.0
    rmat = rmat.astype(bf)
    lmask = np.zeros((128, 8), np.float32)
    for h in range(8):
        if h % 2 == 0:
            lmask[64:128, h] = 1.0
        else:
            lmask[0:64, h] = 1.0
    return dict(cosT=cosT, sinT=sinT, masks=masks, ident=ident, bones=bones, rmat=rmat, lmask=lmask)


def kernel(x, norm_gain, w_in, q_norm_a, k_norm_a, sinks_a, q_norm_b, k_norm_b, w_out):
    x = np.asarray(x, dtype=np.float32)
    if "nc" not in _NC_CACHE:
        _NC_CACHE["nc"] = build_nc()
    nc = _NC_CACHE["nc"]
    cst = _consts()
    f32 = lambda a: np.ascontiguousarray(np.asarray(a, dtype=np.float32))
    gain_b = np.ascontiguousarray(np.broadcast_to(f32(norm_gain).reshape(1, D), (128, D)))
    gvec = np.stack([np.tile(f32(q_norm_a).reshape(64), 2), np.tile(f32(k_norm_a).reshape(64), 2),
                     np.tile(f32(q_norm_b).reshape(64), 2), np.tile(f32(k_norm_b).reshape(64), 2)], axis=1)
    sinks_rep = np.ascontiguousarray(np.broadcast_to(f32(sinks_a).reshape(1, 8), (128, 8)))
    common = dict(w_in=f32(w_in).reshape(D, E_IN), w_out=f32(w_out).reshape(D, D), gain_b=gain_b,
                  gvec=np.ascontiguousarray(gvec), sinks_rep=sinks_rep, **cst)
    in_maps = [dict(x=np.ascontiguousarray(x[i]), **common) for i in range(NCORES)]
    res = run_bass_kernel_spmd(nc, in_maps, core_ids=list(range(NCORES)))
    return np.stack([np.asarray(r["out"], dtype=np.float32) for r in res.results], axis=0)
```
